# Optimizing a Trainium2 kernel written in Bass

```python
import math
import jax, jax.numpy as jnp
from jax import lax
import numpy as np

D_MODEL = 1024
BATCH = 32
SEQ = 256
DEPTH = 2
DEC_BATCH = 8
DEC_SEQ = 1024
PAST_LEN = 512

GRID_W = 64
HEAD_DIM = 64
Q_BLOCK = 128
EPS = 1e-6
ROPE_BASE = 10000.0

LRU_WIDTH = 256
LRU_BLOCKS = 4
LRU_BW = LRU_WIDTH // LRU_BLOCKS
LRU_C = 8.0
CONV_W = 4
CONV_LEFT = 2

GQA_Q_HEADS = 4
GQA_KV_HEADS = 2
GQA_WIDTH = GQA_Q_HEADS * HEAD_DIM
GQA_KV_WIDTH = GQA_KV_HEADS * HEAD_DIM

DIFF_HEADS = 4
DIFF_WIDTH = DIFF_HEADS * 2 * HEAD_DIM

D_MIX = LRU_WIDTH + GQA_WIDTH + DIFF_WIDTH
IN_SIZES = (LRU_WIDTH, LRU_WIDTH, GQA_WIDTH, GQA_KV_WIDTH, GQA_KV_WIDTH, GQA_WIDTH,
            DIFF_WIDTH, DIFF_WIDTH, DIFF_WIDTH, DIFF_WIDTH)
N_IN = sum(IN_SIZES)
IN_OFFSETS = tuple(sum(IN_SIZES[:i + 1]) for i in range(len(IN_SIZES) - 1))

kernel_name = "hybrid_dit_lru_gqa_diffattn_step"


def lambda_init(layer):
    return 0.8 - 0.6 * math.exp(-0.3 * layer)


def rms_norm(x, g):
    xf = x.astype(jnp.float32)
    y = xf * lax.rsqrt(jnp.mean(xf * xf, axis=-1, keepdims=True) + EPS)
    return (y * g.astype(jnp.float32)).astype(x.dtype)


def centred_dwconv(x, w, b):
    T = x.shape[1]
    xp = jnp.pad(x, ((0, 0), (CONV_LEFT, CONV_W - 1 - CONV_LEFT), (0, 0)))
    out = xp[:, 0:T] * w[0] + b
    for j in range(1, CONV_W):
        out = out + xp[:, j:j + T] * w[j]
    return out


def _rotate(x, ang):
    half = x.shape[-1] // 2
    x1, x2 = x[..., :half], x[..., half:]
    cos = jnp.cos(ang)[None, :, None, :]
    sin = jnp.sin(ang)[None, :, None, :]
    return jnp.concatenate([x1 * cos - x2 * sin, x2 * cos + x1 * sin], axis=-1)


def axial_rope(x, n_rows):
    d = x.shape[-1]
    quarter = d // 4
    inv = jnp.power(ROPE_BASE, -jnp.arange(quarter, dtype=jnp.float32) / quarter)
    row = jnp.repeat(jnp.arange(n_rows, dtype=jnp.float32), GRID_W)
    col = jnp.tile(jnp.arange(GRID_W, dtype=jnp.float32), n_rows)
    xf = x.astype(jnp.float32)
    half = d // 2
    out = jnp.concatenate([_rotate(xf[..., :half], row[:, None] * inv[None]),
                           _rotate(xf[..., half:], col[:, None] * inv[None])], axis=-1)
    return out.astype(x.dtype)


def gqa_attend(q, k, v):
    B, Tq, Hq, d = q.shape
    Hkv = k.shape[2]
    G = Hq // Hkv
    nb = Tq // Q_BLOCK
    scale = HEAD_DIM ** -0.5
    qb = q.reshape(B, nb, Q_BLOCK, Hkv, G, d).swapaxes(0, 1)

    def one(qblk):
        s = jnp.einsum('bqhgd,bkhd->bhgqk', qblk, k).astype(jnp.float32) * scale
        p = jax.nn.softmax(s, axis=-1).astype(v.dtype)
        return jnp.einsum('bhgqk,bkhd->bqhgd', p, v)

    o = lax.map(one, qb)
    return o.swapaxes(0, 1).reshape(B, Tq, Hq * v.shape[-1])


def diff_attend(q, k, v, lam):
    B, Tq, H, _, d = q.shape
    nb = Tq // Q_BLOCK
    scale = HEAD_DIM ** -0.5
    qb = q.reshape(B, nb, Q_BLOCK, H, 2, d).swapaxes(0, 1)

    def one(qblk):
        s = jnp.einsum('bqhcd,bkhcd->bhcqk', qblk, k).astype(jnp.float32) * scale
        p = jax.nn.softmax(s, axis=-1)
        w = p[:, :, 0] - lam * p[:, :, 1]
        return jnp.einsum('bhqk,bkhe->bqhe', w.astype(v.dtype), v)

    o = lax.map(one, qb)
    return o.swapaxes(0, 1).reshape(B, Tq, H, v.shape[-1])


def rglru_scan(u, wa, ba, wx, bx, lam, h0, reverse):
    B, T, W = u.shape
    ub = u.reshape(B, T, LRU_BLOCKS, LRU_BW)
    r = jax.nn.sigmoid(jnp.einsum('btnc,ncd->btnd', ub, wa).reshape(B, T, W) + ba)
    i = jax.nn.sigmoid(jnp.einsum('btnc,ncd->btnd', ub, wx).reshape(B, T, W) + bx)
    log_a = -LRU_C * r.astype(jnp.float32) * jax.nn.softplus(-lam.astype(jnp.float32))
    a = jnp.exp(log_a)
    inp = jnp.sqrt(-jnp.expm1(2.0 * log_a)) * (i * u).astype(jnp.float32)

    def step(h, xs):
        a_t, x_t = xs
        h = a_t * h + x_t
        return h, h

    hT, ys = lax.scan(step, h0.astype(jnp.float32),
                      (a.swapaxes(0, 1), inp.swapaxes(0, 1)), reverse=reverse)
    return ys.swapaxes(0, 1).astype(u.dtype), hT.astype(u.dtype)


def mixer(h, p, layer, ctx):
    B, T, _ = h.shape
    z = h @ p['w_in']
    (lru_x, lru_g, gq, gk, gv, gg, dq, dk, dv, dg) = jnp.split(z, IN_OFFSETS, axis=-1)

    u = centred_dwconv(lru_x, p['conv_w'], p['conv_b'])
    if ctx is None:
        h0 = jnp.zeros((B, 2, LRU_WIDTH), dtype=u.dtype)
    else:
        h0 = ctx[4]
    yf, hf = rglru_scan(u, p['wa'][0], p['ba'][0], p['wx'][0], p['bx'][0], p['lam'][0],
                        h0[:, 0], reverse=False)
    yb, hb = rglru_scan(u, p['wa'][1], p['ba'][1], p['wx'][1], p['bx'][1], p['lam'][1],
                        h0[:, 1], reverse=True)
    lru_out = (yf + yb) * jax.nn.silu(lru_g)
    lru_state = jnp.stack([hf, hb], axis=1)

    q = rms_norm(gq.reshape(B, T, GQA_Q_HEADS, HEAD_DIM), p['gqa_gq'])
    k = rms_norm(gk.reshape(B, T, GQA_KV_HEADS, HEAD_DIM), p['gqa_gk'])
    v = gv.reshape(B, T, GQA_KV_HEADS, HEAD_DIM)

    dqr = dq.reshape(B, T, DIFF_HEADS, 2, HEAD_DIM)
    dkr = dk.reshape(B, T, DIFF_HEADS, 2, HEAD_DIM)
    dvr = dv.reshape(B, T, DIFF_HEADS, 2 * HEAD_DIM)

    if ctx is None:
        k_all, v_all, dk_all, dv_all = k, v, dkr, dvr
    else:
        n_rows = T // GRID_W
        q = axial_rope(q, n_rows)
        k_all = jnp.concatenate([ctx[0], axial_rope(k, n_rows)], axis=1)
        v_all = jnp.concatenate([ctx[1], v], axis=1)
        dqr = axial_rope(dqr.reshape(B, T, 2 * DIFF_HEADS, HEAD_DIM), n_rows).reshape(dqr.shape)
        dk_rot = axial_rope(dkr.reshape(B, T, 2 * DIFF_HEADS, HEAD_DIM), n_rows).reshape(dkr.shape)
        dk_all = jnp.concatenate([ctx[2], dk_rot], axis=1)
        dv_all = jnp.concatenate([ctx[3], dvr], axis=1)

    gqa_out = gqa_attend(q, k_all, v_all) * jax.nn.silu(gg)

    dl = p['diff_lam'].astype(jnp.float32)
    lam_i = lambda_init(layer)
    lam = jnp.exp(jnp.sum(dl[0] * dl[1])) - jnp.exp(jnp.sum(dl[2] * dl[3])) + lam_i
    do = diff_attend(dqr, dk_all, dv_all, lam)
    do = rms_norm(do, p['diff_gsub']) * (1.0 - lam_i)
    diff_out = do.reshape(B, T, DIFF_WIDTH) * jax.nn.silu(dg)

    out = jnp.concatenate([lru_out, gqa_out, diff_out], axis=-1) @ p['w_out']
    return out, (k, v, dkr, dvr, lru_state)


def adaln_prenorm(x, cond, w_mod, b_mod, g_pre):
    mod = jax.nn.silu(cond) @ w_mod + b_mod
    shift, scale, gate = jnp.split(mod[:, None, :], 3, axis=-1)
    return rms_norm(x, g_pre) * (1.0 + scale) + shift, gate


def setup_inputs(seed: int = 0) -> dict:
    key = jax.random.key(seed)
    ks = jax.random.split(key, 32)
    f32 = jnp.float32
    nrm = lambda k, s: jax.random.normal(k, s, dtype=f32)
    a_init = jax.random.uniform(ks[20], (DEPTH, 2, LRU_WIDTH), dtype=f32, minval=0.9, maxval=0.999)
    s_init = a_init ** (1.0 / LRU_C)
    return {
        "x_prompt": nrm(ks[0], (BATCH, SEQ, D_MODEL)),
        "x_sample": nrm(ks[1], (DEC_BATCH, DEC_SEQ, D_MODEL)),
        "cache_gqa_k": nrm(ks[2], (DEC_BATCH, DEPTH, PAST_LEN, GQA_KV_HEADS, HEAD_DIM)),
        "cache_gqa_v": nrm(ks[3], (DEC_BATCH, DEPTH, PAST_LEN, GQA_KV_HEADS, HEAD_DIM)),
        "cache_diff_k": nrm(ks[4], (DEC_BATCH, DEPTH, PAST_LEN, DIFF_HEADS, 2, HEAD_DIM)),
        "cache_diff_v": nrm(ks[5], (DEC_BATCH, DEPTH, PAST_LEN, DIFF_HEADS, 2 * HEAD_DIM)),
        "state_lru": 0.5 * nrm(ks[6], (DEC_BATCH, DEPTH, 2, LRU_WIDTH)),
        "c": nrm(ks[7], (DEC_BATCH, D_MODEL)),
        "c_ctx": nrm(ks[8], (D_MODEL,)),
        "w_mod": 0.5 * D_MODEL ** -0.5 * nrm(ks[9], (DEPTH, D_MODEL, 3 * D_MODEL)),
        "b_mod": 0.02 * nrm(ks[10], (DEPTH, 3 * D_MODEL)),
        "g_pre": 1.0 + 0.02 * nrm(ks[11], (DEPTH, D_MODEL)),
        "g_post": 1.0 + 0.02 * nrm(ks[12], (DEPTH, D_MODEL)),
        "w_in": D_MODEL ** -0.5 * nrm(ks[13], (DEPTH, D_MODEL, N_IN)),
        "w_out": D_MIX ** -0.5 * nrm(ks[14], (DEPTH, D_MIX, D_MODEL)),
        "lru_conv_w": CONV_W ** -0.5 * nrm(ks[15], (DEPTH, CONV_W, LRU_WIDTH)),
        "lru_conv_b": 0.02 * nrm(ks[16], (DEPTH, LRU_WIDTH)),
        "lru_wa": LRU_BW ** -0.5 * nrm(ks[17], (DEPTH, 2, LRU_BLOCKS, LRU_BW, LRU_BW)),
        "lru_ba": 0.02 * nrm(ks[18], (DEPTH, 2, LRU_WIDTH)),
        "lru_wx": LRU_BW ** -0.5 * nrm(ks[19], (DEPTH, 2, LRU_BLOCKS, LRU_BW, LRU_BW)),
        "lru_bx": 0.02 * nrm(ks[21], (DEPTH, 2, LRU_WIDTH)),
        "lru_lambda": jnp.log(s_init) - jnp.log1p(-s_init),
        "gqa_gq": 1.0 + 0.02 * nrm(ks[22], (DEPTH, HEAD_DIM)),
        "gqa_gk": 1.0 + 0.02 * nrm(ks[23], (DEPTH, HEAD_DIM)),
        "diff_lam": 0.1 * nrm(ks[24], (DEPTH, 4, HEAD_DIM)),
        "diff_gsub": 1.0 + 0.02 * nrm(ks[25], (DEPTH, 2 * HEAD_DIM)),
    }


def reference(x_prompt, x_sample, cache_gqa_k, cache_gqa_v, cache_diff_k, cache_diff_v, state_lru,
              c, c_ctx, w_mod, b_mod, g_pre, g_post, w_in, w_out, lru_conv_w, lru_conv_b,
              lru_wa, lru_ba, lru_wx, lru_bx, lru_lambda, gqa_gq, gqa_gk, diff_lam, diff_gsub):
    def layer_params(l):
        return {'w_in': w_in[l], 'w_out': w_out[l], 'conv_w': lru_conv_w[l], 'conv_b': lru_conv_b[l],
                'wa': lru_wa[l], 'ba': lru_ba[l], 'wx': lru_wx[l], 'bx': lru_bx[l],
                'lam': lru_lambda[l], 'gqa_gq': gqa_gq[l], 'gqa_gk': gqa_gk[l],
                'diff_lam': diff_lam[l], 'diff_gsub': diff_gsub[l]}

    x = x_prompt
    gks, gvs, dks, dvs, sts = [], [], [], [], []
    for l in range(DEPTH):
        p = layer_params(l)
        h, gate = adaln_prenorm(x, c_ctx[None, :], w_mod[l], b_mod[l], g_pre[l])
        y, (gk, gv, dk, dv, st) = mixer(h, p, l, None)
        x = x + gate * rms_norm(y, g_post[l])
        gks.append(gk); gvs.append(gv); dks.append(dk); dvs.append(dv); sts.append(st)
    y_prompt = x

    x = x_sample
    for l in range(DEPTH):
        p = layer_params(l)
        h, gate = adaln_prenorm(x, c, w_mod[l], b_mod[l], g_pre[l])
        ctx = (cache_gqa_k[:, l], cache_gqa_v[:, l], cache_diff_k[:, l], cache_diff_v[:, l],
               state_lru[:, l])
        y, _ = mixer(h, p, l, ctx)
        x = x + gate * rms_norm(y, g_post[l])
    y_sample = x

    new_gqa_k = jnp.stack(gks, axis=1)
    new_gqa_v = jnp.stack(gvs, axis=1)
    new_diff_k = jnp.stack(dks, axis=1)
    new_diff_v = jnp.stack(dvs, axis=1)
    new_lru = jnp.stack(sts, axis=1)
    return (y_prompt, y_sample, new_gqa_k, new_gqa_v, new_diff_k, new_diff_v, new_lru)
```

```python
import math
import os
from contextlib import ExitStack

import numpy as np
import concourse.bass as bass
import concourse.mybir as mybir
from concourse.alu_op_type import AluOpType as ALU
from concourse.ap import AP
from concourse.bass_utils import run_bass_kernel_spmd

F32 = mybir.dt.float32
BF16 = mybir.dt.bfloat16
AF = mybir.ActivationFunctionType
AX = mybir.AxisListType

EPS = 1e-6
NPV = 150
N_CORES = 8
ENGS = ["sync", "scalar", "gpsimd", "vector", "tensor"]
N_DMA_SEMS = 64
N_HW_SEMS = 40
SAME_ENGINE_SYNC = True


class Res:
    __slots__ = ("w", "r", "x")

    def __init__(self, x=False):
        self.w = None
        self.r = {}
        self.x = x


class Prog:
    def __init__(self):
        self.q = {e: [] for e in ENGS}
        self.cnt = {e: 0 for e in ENGS}
        self.dma_val = [0] * N_DMA_SEMS
        self.dma_next = {}
        self.waited = {e: {} for e in ENGS}
        self.out_tokens = []
        self.last_sig = {e: True for e in ENGS}

    def op(self, eng, fn, reads=(), writes=(), dma=False, is_output=False, signal=True):
        deps = {}
        same = SAME_ENGINE_SYNC and eng != "tensor"

        def add(tok, same_ok):
            if tok is None:
                return
            k, v = tok
            if k == ("c", eng) and not same_ok:
                return
            if deps.get(k, 0) < v:
                deps[k] = v

        for r in reads:
            add(r.w, same)
            if r.x:
                for k, v in r.r.items():
                    add((k, v), False)
        for w in writes:
            add(w.w, same)
            for k, v in w.r.items():
                add((k, v), same)
        if dma:
            lo, hi = (0, N_HW_SEMS) if eng == "sync" else (N_HW_SEMS, N_DMA_SEMS)
            si = self.dma_next.get(eng, lo)
            self.dma_next[eng] = lo + (si + 1 - lo) % (hi - lo)
            if self.dma_val[si] > 0:
                k = ("d", si)
                if deps.get(k, 0) < self.dma_val[si]:
                    deps[k] = self.dma_val[si]
            self.dma_val[si] += 16
            tok = (("d", si), self.dma_val[si])
        elif signal:
            self.cnt[eng] += 1
            tok = (("c", eng), self.cnt[eng])
        else:
            tok = (("c", eng), self.cnt[eng] + 1)
        self.last_sig[eng] = signal or dma
        waits = []
        wd = self.waited[eng]
        for k, v in deps.items():
            if wd.get(k, 0) >= v:
                continue
            wd[k] = v
            waits.append((k, v))
        k, v = tok
        for r in reads:
            if r.r.get(k, 0) < v:
                r.r[k] = v
        for w in writes:
            w.w = tok
            w.r = {}
        self.q[eng].append((waits, fn, tok, signal))
        if is_output:
            self.out_tokens.append(tok)
        return tok

    def emit(self, nc, st):
        csem = {e: st.enter_context(nc.semaphore("c_" + e)) for e in ENGS}
        dsem = [st.enter_context(nc.semaphore("d_%d" % i)) for i in range(N_DMA_SEMS)]
        block = st.enter_context(nc.Block())

        def sem_of(k):
            return csem[k[1]] if k[0] == "c" else dsem[k[1]]

        def run(engname, engine):
            assert self.last_sig[engname], engname
            for waits, fn, tok, signal in self.q[engname]:
                for k, v in waits:
                    engine.wait_ge(sem_of(k), v)
                ins = fn(engine)
                k, v = tok
                if k[0] == "c":
                    if signal:
                        ins.then_inc(csem[k[1]], 1)
                else:
                    ins.then_inc(dsem[k[1]], 16)
            if engname == "sync":
                final = {}
                for k, v in self.out_tokens:
                    if final.get(k, 0) < v:
                        final[k] = v
                for k, v in final.items():
                    engine.wait_ge(sem_of(k), v)

        @block.sync
        def _(e):
            run("sync", e)

        @block.scalar
        def _(e):
            run("scalar", e)

        @block.gpsimd
        def _(e):
            run("gpsimd", e)

        @block.vector
        def _(e):
            run("vector", e)

        @block.tensor
        def _(e):
            run("tensor", e)


def ap_bc_mid(ap, n):
    a = [list(x) for x in ap.ap]
    return AP(ap.tensor, ap.offset, [a[0], [0, n]] + a[1:])


def ap_bc_last(ap, n):
    a = [list(x) for x in ap.ap]
    return AP(ap.tensor, ap.offset, a + [[0, n]])


def lambda_init(layer):
    return 0.8 - 0.6 * math.exp(-0.3 * layer)


C_LRUX, C_LRUG, C_GQ, C_GK, C_GV, C_GG, C_DQ, C_DK, C_DV, C_DG = (
    0, 256, 512, 768, 896, 1024, 1280, 1792, 2304, 2816)


def build():
    nc = bass.Bass("TRN2", target_bir_lowering=False)
    P = Prog()
    st = ExitStack()

    def din(name, shape):
        return nc.dram_tensor(name, list(shape), F32, kind="ExternalInput").ap()

    def dout(name, shape):
        return nc.dram_tensor(name, list(shape), F32, kind="ExternalOutput").ap()

    xin = [din("xp", [1024, 1024]), din("xs", [1024, 1024])]
    cgk = din("cgk", [2, 512, 128]); cgv = din("cgv", [2, 512, 128])
    cdk = din("cdk", [2, 512, 512]); cdv = din("cdv", [2, 512, 512])

    w_mod = din("w_mod", [2, 1024, 3072])
    w_in = din("w_in", [2, 1024, 3328]); w_out = din("w_out", [2, 1024, 1024])
    gq = din("gqa_gq", [2, 64]); gk = din("gqa_gk", [2, 64])
    dlam = din("diff_lam", [2, 4, 64])
    ident = din("ident", [128, 128]); ropec = din("ropec", [1024, 64]); ropes = din("ropes", [1024, 64])
    pvec = din("pvec", [128, NPV]); wbd_in = din("wbd", [128, 2 * 8 * 128])

    yout = [dout("yp", [1024, 1024]), dout("ys", [1024, 1024])]
    ngk = dout("ngk", [4, 2, 256, 128]); ngv = dout("ngv", [4, 2, 256, 128])
    ndk = dout("ndk", [4, 2, 256, 512]); ndv = dout("ndv", [4, 2, 256, 512])
    nlru = dout("nlru", [4, 2, 2, 256])

    def sb(name, shape, dt):
        return st.enter_context(nc.sbuf_tensor(name, list(shape), dt))

    X = sb("X", [128, 8, 1024], F32); RX = [Res() for _ in range(8)]
    WIN = sb("WIN", [128, 8, 3328], BF16)
    RW_LRU = Res(); RW_GQA = Res(); RW_DIFF = [Res() for _ in range(4)]
    WOUT = sb("WOUT", [128, 8, 1024], BF16); RW_OUT = Res()
    HT = sb("HT", [128, 8, 1024], BF16); RHT = [Res() for _ in range(8)]
    MIXT = sb("MIXT", [128, 8, 1024], BF16); RMIX = [[Res() for _ in range(8)] for _ in range(8)]

    IDF = sb("IDF", [128, 128], F32); IDB = sb("IDB", [128, 128], BF16); R_ID = Res()
    ROPEC = sb("ROPEC", [128, 8, 64], F32); ROPES = sb("ROPES", [128, 8, 64], F32); R_ROPE = Res()
    PVEC = sb("PVEC", [128, NPV], F32)
    _o = [0]

    def pv(n, pattern=None, **kw):
        v = PVEC[:, _o[0]:_o[0] + n]
        _o[0] += n
        return v.rearrange(pattern, **kw) if pattern else v
    CONDT = pv(16, "p (k c) -> p k c", c=2)
    BMT = pv(48, "p (l k) -> p l k", l=2)
    GPRE = pv(16, "p (l k) -> p l k", l=2)
    GPOST = pv(16, "p (l k) -> p l k", l=2)
    CW = pv(16, "p (l c j) -> p l c j", l=2, c=2)
    CB = pv(4, "p (l c) -> p l c", l=2)
    BA = pv(8, "p (l d c) -> p l d c", l=2, d=2)
    BXX = pv(8, "p (l d c) -> p l d c", l=2, d=2)
    LAMT = pv(8, "p (l d c) -> p l d c", l=2, d=2)
    H0 = pv(8, "p (l d c) -> p l d c", l=2, d=2)
    GSUBS = pv(2)
    assert _o[0] == NPV
    SC = sb("SC", [128, 8, 2], BF16); R_SC = Res()
    MODT = sb("MODT", [128, 2, 24, 2], F32); GS = sb("GS", [128, 2, 8, 2], F32); GPT = sb("GPT", [128, 2, 8, 2], F32)
    R_MOD = [Res(), Res()]
    CEXP = sb("CEXP", [128, 2, 2, 2], F32)
    WBD = sb("WBD", [128, 2, 8, 128], BF16)
    G6 = sb("G6", [128, 2, 6, 64], F32)
    DLS = sb("DLS", [128, 2, 4], F32)
    NLAM = sb("NLAM", [128, 2], F32)
    PAR = []

    def pres():
        r = Res()
        PAR.append(r)
        return r
    STP = sb("STP", [128, 32, 24], F32); RST = [Res() for _ in range(32)]
    st_i = [0]

    def new_stat():
        i = st_i[0] % 32
        st_i[0] += 1
        return STP[:, i, :], RST[i]

    GR = 64
    A_COLS = 15040
    ARENA = sb("ARENA", [128, A_COLS], F32)
    RGR = [Res() for _ in range(A_COLS // GR)]
    bump = [0]

    class Tile:
        def __init__(self, off, ncols):
            assert off % GR == 0
            self.off = off
            self.ncols = ncols
            self.res = RGR[off // GR:(off + ncols + GR - 1) // GR]

        def f32(self):
            return ARENA[:, self.off:self.off + self.ncols]

        def bf(self):
            return ARENA[:, self.off:self.off + self.ncols].bitcast(BF16)

    def alloc(ncols):
        off = bump[0]
        n = ((ncols + GR - 1) // GR) * GR
        bump[0] += n
        assert bump[0] <= A_COLS, bump[0]
        return Tile(off, ncols)

    XPAD = alloc(1088); U = alloc(1024); RR = alloc(1024); II = alloc(1024); TT = alloc(1024)
    Y0 = alloc(1024); Y1 = alloc(1024); UBF = alloc(512)
    XN = U; JUNK = Tile(RR.off, 512)
    GPB = II; TMP = TT; JUNK2 = Tile(Y0.off, 512); BCT = [Tile(Y1.off, 128), Tile(Y1.off + 128, 128)]
    QT = alloc(1024)
    KT = alloc(768)
    VA = alloc(1024)
    GATE = alloc(1024)
    assert KT.off == QT.off + 1024 and VA.off == QT.off + 1792 and GATE.off == QT.off + 2816
    PTB = [alloc(256) for _ in range(3)]
    QK = alloc(384); T1 = alloc(384)
    QKB = [alloc(192), alloc(192)]
    T2 = alloc(384)
    CKB = alloc(256)
    assert CKB.off == T2.off + 384
    OST = [Tile(T2.off, 256), Tile(T2.off + 256, 256)]
    O0T = alloc(256); DOT = alloc(256)
    DOBT = alloc(128)
    OBT = alloc(256)
    assert T1.off == QK.off + 384 and CKB.off == T2.off + 384
    assert DOT.off == O0T.off + 256 and DOBT.off == O0T.off + 512
    GQT = [(QK, T1, T2), (Tile(GATE.off, 384), Tile(GATE.off + 384, 384), Tile(O0T.off, 384))]
    T1S = [Tile(QK.off, 256), Tile(QK.off + 256, 256)]
    T2S = [Tile(T2.off, 256), Tile(T2.off + 256, 256)]
    WM = [Tile(QT.off, 2048), Tile(QT.off + 2048, 2048), Tile(QT.off + 4096, 2048)]
    WBDF_T = Tile(U.off, 2048); DLB_T = Tile(II.off, 512); DLP_T = Tile(II.off + 512, 256)
    GQB_T = Tile(II.off + 768, 128); GKB_T = Tile(II.off + 896, 128)
    WBDF = WBDF_T.f32().rearrange("p (l i c) -> p l i c", l=2, i=8)
    DLB = DLB_T.f32().rearrange("p (l r d) -> p l r d", l=2, r=4)
    DLP = DLP_T.f32().rearrange("p (l r d) -> p l r d", l=2, r=2)
    GQB = GQB_T.f32().rearrange("p (l d) -> p l d", l=2)
    GKB = GKB_T.f32().rearrange("p (l d) -> p l d", l=2)

    PS = st.enter_context(nc.psum_tensor("PS", [128, 4096], F32))
    RPB = [Res(x=True) for _ in range(8)]

    def bank(b):
        return PS[:, b * 512:(b + 1) * 512]

    def bankbf(b):
        return PS[:, b * 512:(b + 1) * 512].bitcast(BF16)

    def dbank(d):
        return PS[:, d * 1024:(d + 1) * 1024]

    def rdb(d):
        return [RPB[2 * d], RPB[2 * d + 1]]

    def V(fn, r=(), w=()):
        return P.op("vector", fn, r, w)

    def A(fn, r=(), w=()):
        return P.op("scalar", fn, r, w)

    def G(fn, r=(), w=()):
        return P.op("gpsimd", fn, r, w)

    def T(fn, r=(), w=(), sig=True):
        return P.op("tensor", fn, r, w, signal=sig)

    def D(fn, r=(), w=(), out=False):
        return P.op("sync", fn, r, w, dma=True, is_output=out)

    def DG(fn, r=(), w=()):
        return P.op("gpsimd", fn, r, w, dma=True)

    def load_win_lru(l):
        src = w_in[l].rearrange("(kc p) c -> p kc c", p=128)
        DG(lambda e: e.dma_start(out=WIN[:, :, 0:512], in_=src[:, :, 0:512]), w=[RW_LRU])

    def load_win_gqa(l):
        src = w_in[l].rearrange("(kc p) c -> p kc c", p=128)
        DG(lambda e: e.dma_start(out=WIN[:, :, 512:1280], in_=src[:, :, 512:1280]), w=[RW_GQA])

    def load_win_diff(l, h):
        src = w_in[l].rearrange("(kc p) c -> p kc c", p=128)
        for seg in range(4):
            c0 = C_DQ + seg * 512 + h * 128
            DG(lambda e, c0=c0: e.dma_start(out=WIN[:, :, c0:c0 + 128], in_=src[:, :, c0:c0 + 128]), w=[RW_DIFF[h]])

    def load_wout(l):
        src = w_out[l].rearrange("(kc p) c -> p kc c", p=128)
        DG(lambda e: e.dma_start(out=WOUT[:], in_=src), w=[RW_OUT])

    r_pv = pres()
    D(lambda e: e.dma_start(out=PVEC[:], in_=pvec), w=[r_pv])
    D(lambda e: e.dma_start(out=IDF[:], in_=ident), w=[R_ID])
    DG(lambda e: e.dma_start(out=IDB[:], in_=ident), w=[R_ID])
    A(lambda e: e.activation(SC[:], CONDT, AF.Silu), r=[r_pv], w=[R_SC])
    for tt in range(8):
        D(lambda e, tt=tt: e.dma_start(out=X[:, tt, :], in_=xin[0][tt * 128:(tt + 1) * 128, :]), w=[RX[tt]])
    D(lambda e: e.dma_start(out=ROPEC[:], in_=ropec.rearrange("(t p) d -> p t d", p=128)), w=[R_ROPE])
    D(lambda e: e.dma_start(out=ROPES[:], in_=ropes.rearrange("(t p) d -> p t d", p=128)), w=[R_ROPE])

    def small_load(out_ap, in_ap, extra=()):
        r = pres()
        D(lambda e: e.dma_start(out=out_ap, in_=in_ap, allow_slow_non_contiguous=True), w=[r] + list(extra))
        return r

    r_gqb = []; r_gkb = []; r_dlb = []
    for l in range(2):
        r_gqb.append(small_load(GQB[:, l, :], gq[l].partition_broadcast(128), GQB_T.res))
        r_gkb.append(small_load(GKB[:, l, :], gk[l].partition_broadcast(128), GKB_T.res))
        r_dlb.append(small_load(DLB[:, l, :, :], dlam[l].partition_broadcast(128), DLB_T.res))
    r_wbd = pres()
    DG(lambda e: e.dma_start(out=WBD[:].rearrange("p l i c -> p (l i c)"), in_=wbd_in), w=[r_wbd])
    r_cexp = pres()
    A(lambda e: e.activation(CEXP[:], LAMT, AF.Exp, scale=-1.0), r=[r_pv], w=[r_cexp])
    A(lambda e: e.activation(CEXP[:], CEXP[:], AF.Ln, scale=1.0, bias=1.0), r=[r_cexp], w=[r_cexp])
    V(lambda e: e.tensor_scalar(out=CEXP[:], in0=CEXP[:], scalar1=-8.0, scalar2=None, op0=ALU.mult), r=[r_cexp], w=[r_cexp])
    r_g6 = pres()
    for l in range(2):
        V(lambda e, l=l: e.tensor_copy(G6[:, l, 0:4, :], ap_bc_mid(GQB[:, l, :], 4)), r=r_gqb + GQB_T.res, w=[r_g6])
        V(lambda e, l=l: e.tensor_copy(G6[:, l, 4:6, :], ap_bc_mid(GKB[:, l, :], 2)), r=r_gkb + GKB_T.res, w=[r_g6])
    r_nl = pres()
    V(lambda e: e.tensor_tensor(DLP, DLB[:, :, 0::2, :], DLB[:, :, 1::2, :], ALU.mult), r=r_dlb + DLB_T.res, w=DLP_T.res)
    V(lambda e: e.tensor_reduce(out=DLS[:, :, 0:2], in_=DLP, axis=AX.X, op=ALU.add), r=DLP_T.res, w=[r_nl])
    A(lambda e: e.activation(DLS[:, :, 2:4], DLS[:, :, 0:2], AF.Exp), r=[r_nl], w=[r_nl])
    V(lambda e: e.tensor_tensor(NLAM[:], DLS[:, :, 3], DLS[:, :, 2], ALU.subtract), r=[r_nl], w=[r_nl])
    for l in range(2):
        V(lambda e, l=l: e.tensor_scalar(out=NLAM[:, l:l + 1], in0=NLAM[:, l:l + 1], scalar1=-lambda_init(l),
                                         scalar2=None, op0=ALU.add), r=[r_nl], w=[r_nl])
        V(lambda e, l=l: e.tensor_scalar(out=GSUBS[:, l:l + 1], in0=GSUBS[:, l:l + 1], scalar1=1.0 - lambda_init(l),
                                         scalar2=None, op0=ALU.mult), r=[r_pv], w=[r_pv])

    load_win_lru(0)
    wmi = [0]

    def mod(l, wms, psm_b):
        PSM = bank(psm_b)[:, 0:48].rearrange("p (f c) -> p f c", c=2)
        srcs = [w_mod[l].rearrange("(kc p) c -> p kc c", p=128)[:, :, blk * 512:(blk + 1) * 512] for blk in range(6)]

        def dma(blk):
            wm = wms[blk % len(wms)]
            DG(lambda e: e.dma_start(out=wm.bf().rearrange("p (k c) -> p k c", k=8), in_=srcs[blk]), w=wm.res)

        for blk in range(len(wms)):
            dma(blk)
        yield
        for blk in range(6):
            wm = wms[blk % len(wms)]
            wmv = wm.bf().rearrange("p (k c) -> p k c", k=8)
            for fcl in range(4):
                fc = blk * 4 + fcl
                for kc in range(8):
                    T(lambda e, fc=fc, kc=kc, fcl=fcl, wmv=wmv: e.matmul(
                        PSM[:, fc, :], wmv[:, kc, fcl * 128:(fcl + 1) * 128], SC[:, kc, :],
                        start=(kc == 0), stop=(kc == 7)), r=wm.res + [R_SC], w=[RPB[psm_b]], sig=(kc == 7))
            if blk + len(wms) < 6:
                dma(blk + len(wms))
            yield
        V(lambda e: e.tensor_tensor(MODT[:, l], PSM, ap_bc_last(BMT[:, l, :], 2), ALU.add),
          r=[RPB[psm_b]] + PAR, w=[R_MOD[l]])
        V(lambda e: e.scalar_tensor_tensor(out=GS[:, l], in0=MODT[:, l, 8:16, :], scalar=1.0,
                                           in1=ap_bc_last(GPRE[:, l, :], 2), op0=ALU.add, op1=ALU.mult),
          r=[R_MOD[l]] + PAR, w=[R_MOD[l]])
        V(lambda e: e.tensor_tensor(GPT[:, l], MODT[:, l, 16:24, :], ap_bc_last(GPOST[:, l, :], 2), ALU.mult),
          r=[R_MOD[l]] + PAR, w=[R_MOD[l]])
        yield

    for _ in mod(0, WM, 6):
        pass
    load_win_gqa(0)
    for h in range(4):
        load_win_diff(0, h)
    load_wout(0)

    steps = [(0, 0), (0, 1), (1, 0), (1, 1)]

    def rstd_from_sumsq(stat, rs, n_el, cols=1):
        A(lambda e: e.activation(stat[:, 8:8 + cols], stat[:, 0:cols], AF.Ln, scale=1.0 / n_el, bias=EPS), r=[rs], w=[rs])
        A(lambda e: e.activation(stat[:, 16:16 + cols], stat[:, 8:8 + cols], AF.Exp, scale=-0.5), r=[rs], w=[rs])

    def x_load(s, tt):
        DG(lambda e: e.dma_start(out=X[:, tt, :], in_=xin[s][tt * 128:(tt + 1) * 128, :]), w=[RX[tt]])

    def prenorm_tile(s, l, tt):
        c = s
        stat, rs = new_stat()
        A(lambda e: e.activation(JUNK.bf(), X[:, tt, :], AF.Square, accum_out=stat[:, 0:1]), r=[RX[tt]], w=JUNK.res + [rs])
        yield
        rstd_from_sumsq(stat, rs, 1024)
        yield
        V(lambda e: e.tensor_scalar(out=XN.f32(), in0=X[:, tt, :], scalar1=stat[:, 16:17], scalar2=None, op0=ALU.mult),
          r=[RX[tt], rs], w=XN.res)
        yield
        d = 2 + tt % 2
        pst = dbank(d)
        for kc in range(8):
            T(lambda e, kc=kc: e.transpose(pst[:, kc * 128:(kc + 1) * 128], XN.f32()[:, kc * 128:(kc + 1) * 128], IDF[:]),
              r=XN.res + [R_ID], w=[RPB[2 * d + kc // 4]], sig=(kc % 4 == 3))
        yield
        for kc in range(8):
            rr = [RPB[2 * d + kc // 4], R_MOD[l]]
            if kc < 4:
                A(lambda e, kc=kc: e.activation(HT[:, kc, tt * 128:(tt + 1) * 128], pst[:, kc * 128:(kc + 1) * 128], AF.Identity,
                                                scale=GS[:, l, kc, c:c + 1], bias=MODT[:, l, kc, c:c + 1]), r=rr, w=[RHT[tt]])
            else:
                V(lambda e, kc=kc: e.tensor_scalar(out=HT[:, kc, tt * 128:(tt + 1) * 128], in0=pst[:, kc * 128:(kc + 1) * 128],
                                                   scalar1=GS[:, l, kc, c:c + 1], scalar2=MODT[:, l, kc, c:c + 1],
                                                   op0=ALU.mult, op1=ALU.add), r=rr, w=[RHT[tt]])
            if kc % 2 == 1:
                yield

    def prenorm(s, l):
        for tt in range(8):
            for _ in prenorm_tile(s, l, tt):
                pass

    def proj_fm(col0, d, rw):
        ps = dbank(d)
        for half in range(2):
            for kc in range(8):
                T(lambda e, half=half, kc=kc: e.matmul(ps[:, half * 512:(half + 1) * 512], WIN[:, kc, col0:col0 + 128],
                                                       HT[:, kc, half * 512:(half + 1) * 512], start=(kc == 0), stop=(kc == 7)),
                  r=[rw] + RHT[half * 4:half * 4 + 4], w=[RPB[2 * d + half]], sig=(kc == 7))
        return ps

    def all_gates(s, l):
        cols = [(C_LRUG, RW_LRU), (C_LRUG + 128, RW_LRU), (C_GG, RW_GQA), (C_GG + 128, RW_GQA)] + \
               [(C_DG + h * 128, RW_DIFF[h]) for h in range(4)]
        for fc, (col0, rw) in enumerate(cols):
            d = fc % 2
            ps = proj_fm(col0, d, rw)
            yield
            A(lambda e, fc=fc, ps=ps: e.activation(MIXT[:, fc, :], ps, AF.Silu), r=rdb(d), w=RMIX[fc])
            yield

    def par_gen(*gens):
        alive = list(gens)
        while alive:
            for g in list(alive):
                try:
                    next(g)
                except StopIteration:
                    alive.remove(g)
            yield

    def lru(s, l):
        nseq, L = (4, 256) if s == 0 else (1, 1024)
        xpad = XPAD.f32()[:, 0:nseq * (L + 3)].rearrange("p (q t) -> p q t", q=nseq)
        XR = Tile(XPAD.off, 1024)
        bufs = [(RR, II, Y0, 1), (XR, TT, Y1, 0)]

        def v3(t):
            return t.f32().rearrange("p (q t) -> p q t", q=nseq)

        for ch in range(2):
            G(lambda e: e.memset(xpad[:, :, 0:2], 0.0), w=XPAD.res)
            G(lambda e: e.memset(xpad[:, :, L + 2:L + 3], 0.0), w=XPAD.res)
            yield
            ps = proj_fm(C_LRUX + ch * 128, 0, RW_LRU)
            yield
            A(lambda e, ps=ps: e.copy(xpad[:, :, 2:2 + L], ps.rearrange("p (q t) -> p q t", q=nseq)), r=rdb(0), w=XPAD.res)
            yield
            V(lambda e, ch=ch: e.tensor_scalar(out=v3(U), in0=xpad[:, :, 0:L], scalar1=CW[:, l, ch, 0:1],
                                               scalar2=CB[:, l, ch:ch + 1], op0=ALU.mult, op1=ALU.add),
              r=XPAD.res + PAR, w=U.res)
            yield
            for j in range(1, 4):
                V(lambda e, ch=ch, j=j: e.scalar_tensor_tensor(out=v3(U), in0=xpad[:, :, j:j + L], scalar=CW[:, l, ch, j:j + 1],
                                                               in1=v3(U), op0=ALU.mult, op1=ALU.add),
                  r=XPAD.res + U.res + PAR, w=U.res)
                yield
            A(lambda e: e.copy(UBF.bf(), U.f32()), r=U.res, w=UBF.res)
            yield

            def direction(d, ch=ch):
                R, I, Y, pd = bufs[d]
                for typ in range(2):
                    wi = d * 4 + typ * 2 + ch
                    for half in range(2):
                        T(lambda e, wi=wi, half=half: e.matmul(
                            dbank(pd)[:, half * 512:(half + 1) * 512], WBD[:, l, wi, :], UBF.bf()[:, half * 512:(half + 1) * 512],
                            start=True, stop=True), r=UBF.res + PAR, w=[RPB[2 * pd + half]])
                    yield
                    dst = R if typ == 0 else I
                    bias = (BA if typ == 0 else BXX)[:, l, d, ch:ch + 1]
                    A(lambda e, dst=dst, bias=bias: e.activation(dst.f32(), dbank(pd), AF.Sigmoid, bias=bias),
                      r=rdb(pd) + PAR, w=dst.res)
                    yield
                A(lambda e: e.activation(R.f32(), R.f32(), AF.Exp, scale=CEXP[:, l, d, ch:ch + 1]), r=R.res + PAR, w=R.res)
                yield
                V(lambda e: e.tensor_tensor(Y.f32(), R.f32(), R.f32(), ALU.mult), r=R.res, w=Y.res)
                yield
                A(lambda e: e.activation(Y.f32(), Y.f32(), AF.Sqrt, scale=-1.0, bias=1.0), r=Y.res, w=Y.res)
                yield
                if s == 0:
                    pos = 0 if d == 0 else L - 1
                    V(lambda e, pos=pos: e.memset(v3(R)[:, :, pos:pos + 1], 0.0), r=R.res, w=R.res)
                V(lambda e: e.tensor_tensor(I.f32(), I.f32(), U.f32(), ALU.mult), r=I.res + U.res, w=I.res)
                yield
                V(lambda e: e.tensor_tensor(I.f32(), I.f32(), Y.f32(), ALU.mult), r=I.res + Y.res, w=I.res)
                yield
                init = 0.0 if s == 0 else H0[:, l, d, ch:ch + 1]
                if d == 0:
                    V(lambda e: e.tensor_tensor_scan(Y.f32(), R.f32(), I.f32(), init, ALU.mult, ALU.add),
                      r=R.res + I.res + PAR, w=Y.res)
                else:
                    V(lambda e: e.tensor_tensor_scan(Y.f32()[:, ::-1], R.f32()[:, ::-1], I.f32()[:, ::-1],
                                                     init, ALU.mult, ALU.add), r=R.res + I.res + PAR, w=Y.res)
                yield
                if s == 0:
                    pos = L - 1 if d == 0 else 0
                    srcv = v3(Y)[:, :, pos]
                    dstv = nlru[:, l, d, ch * 128:(ch + 1) * 128].rearrange("q p -> p q")
                    D(lambda e: e.dma_start(out=dstv, in_=srcv, allow_slow_non_contiguous=True), r=Y.res, out=True)

            yield from par_gen(direction(0), direction(1))
            V(lambda e: e.tensor_tensor(Y0.f32(), Y0.f32(), Y1.f32(), ALU.add), r=Y0.res + Y1.res, w=Y0.res)
            yield
            V(lambda e, ch=ch: e.tensor_tensor(MIXT[:, ch, :], Y0.f32(), MIXT[:, ch, :], ALU.mult), r=Y0.res + RMIX[ch], w=RMIX[ch])
            yield

    pt_i = [0]

    def rope_views(src_ap, nh, tt):
        x5 = src_ap.rearrange("p (h b f d) -> p h b f d", h=nh, b=2, f=2)
        xs5 = x5[:, :, :, ::-1, :]
        c5 = ap_bc_mid(ROPEC[:, tt, :].rearrange("p (b f d) -> p b f d", b=2, f=2), nh)
        s5 = ap_bc_mid(ROPES[:, tt, :].rearrange("p (b f d) -> p b f d", b=2, f=2), nh)
        return x5, xs5, c5, s5

    PJ_B, TR_B = 6, 7

    def interleave(gens, weights=None):
        alive = [[g, (weights[i] if weights else 1)] for i, g in enumerate(gens) if g is not None]
        while alive:
            for item in list(alive):
                g, w = item
                for _ in range(w):
                    try:
                        next(g)
                    except StopIteration:
                        alive.remove(item)
                        break

    def attn_core(s, qb, qt_ap, kt_ap, va_of, W, rd_extra):
        chunks = [2 * qb, 2 * qb + 1] if s == 0 else list(range(12))
        n = len(chunks)
        W1 = W + 1
        pts = {}

        def scores(ci):
            kc = chunks[ci]
            b0 = 2 if s == 0 else 2 + 2 * (ci % 2)
            for m in range(2):
                T(lambda e, m=m, kc=kc, b0=b0: e.matmul(bank(b0 + m)[:, 0:256], kt_ap[m * 64:(m + 1) * 64, kc, :],
                                                        qt_ap[m * 64:(m + 1) * 64, qb * 256:(qb + 1) * 256], start=True, stop=True),
                  r=rd_extra, w=[RPB[b0 + m]], sig=(m == 1))
            pt = PTB[pt_i[0] % len(PTB)]; pt_i[0] += 1
            pts[ci] = pt
            sc2 = PS[:, b0 * 512:(b0 + 2) * 512].rearrange("p (m c) -> p m c", m=2)[:, :, 0:256]
            A(lambda e, sc2=sc2, pt=pt: e.activation(pt.bf().rearrange("p (m c) -> p m c", m=2), sc2, AF.Exp, scale=0.125),
              r=[RPB[b0], RPB[b0 + 1]], w=pt.res)

        def pv(ci):
            kc = chunks[ci]
            pt = pts[ci]
            for m in range(2):
                for qt in range(2):
                    last = (m == 1 and qt == 1)
                    T(lambda e, m=m, qt=qt, kc=kc, pt=pt, ci=ci: e.matmul(
                        bank(m)[:, qt * W1:(qt + 1) * W1], pt.bf()[:, m * 256 + qt * 128:m * 256 + (qt + 1) * 128],
                        va_of(m)[:, kc, 0:W1], start=(ci == 0 and qt == 0), stop=(ci == n - 1),
                        skip_group_check=True), r=pt.res + rd_extra, w=[RPB[m]], sig=last)

        scores(0)
        yield
        for ci in range(n):
            if ci + 1 < n:
                scores(ci + 1)
            pv(ci)
            yield

    class BufSet:
        def __init__(self, base):
            self.QK = Tile(base, 1792)
            self.VA = Tile(base + 1792, 1024)
            self.GATE = Tile(base + 2816, 1024)

    BS_A = BufSet(QT.off)
    BS_B = BufSet(II.off)
    _gb = II.off + 2816
    QKBD = QKB + [Tile(_gb, 192), Tile(_gb + 192, 192)]
    OSTD = OST + [Tile(_gb + 384, 256), Tile(_gb + 640, 256)]

    def proj_gate(col0, rw, dst_of_half, gres):
        for half in range(2):
            for kc in range(8):
                T(lambda e, half=half, kc=kc: e.matmul(bank(PJ_B), WIN[:, kc, col0:col0 + 128],
                                                       HT[:, kc, half * 512:(half + 1) * 512], start=(kc == 0), stop=(kc == 7)),
                  r=[rw] + RHT[half * 4:half * 4 + 4], w=[RPB[PJ_B]], sig=(kc == 7))
            yield
            A(lambda e, half=half: e.activation(dst_of_half(half), bank(PJ_B), AF.Silu), r=[RPB[PJ_B]], w=gres)
            yield

    def cache_k_load(src_ap):
        DG(lambda e: e.dma_start(out=CKB.bf().rearrange("p (c d) -> p c d", c=4), in_=src_ap), w=CKB.res)
        ptr = bankbf(PJ_B)
        for c4 in range(4):
            T(lambda e, c4=c4: e.transpose(ptr[:, c4 * 128:(c4 + 1) * 128], CKB.bf()[:, c4 * 128:(c4 + 1) * 128], IDB[:]),
              r=CKB.res + [R_ID], w=[RPB[PJ_B]], sig=(c4 == 3))
        return ptr

    def gqa_unit(s, l, bs):
        koff = 0 if s == 0 else 4
        qkall = bs.QK.bf()
        qt2 = qkall[:, 0:2048].rearrange("p (u t) -> p u t", u=2)
        kt = qkall[:, 2048:2048 + 1536].rearrange("p (k t) -> p k t", t=128)
        va = bs.VA.bf()[:, 0:2 * 12 * 66].rearrange("p (v k w) -> p v k w", v=2, k=12)
        allres = bs.QK.res + bs.VA.res

        def prep_head():
            G(lambda e: e.memset(va[:, :, :, 64:65], 1.0), w=bs.VA.res)
            yield
            if s == 1:
                for kv in range(2):
                    DG(lambda e, kv=kv: e.dma_start(out=va[:, kv, 0:4, 0:64],
                                                    in_=cgv[l].rearrange("(c p) (v d) -> p v c d", p=128, v=2)[:, kv]), w=bs.VA.res)
                ptr = cache_k_load(cgk[l].rearrange("(c p) d -> p c d", p=128))
                yield
                A(lambda e: e.copy(qkall[:, 2048:2048 + 512], ptr[:, 0:512]), r=[RPB[PJ_B]], w=bs.QK.res)
                yield

        def prep_tile(tt):
            par = tt % 2
            PJ = 6 if par == 0 else 4
            TRB = PJ + 1
            pj = bank(PJ)
            QK, T1, T2 = GQT[par]
            for kc in range(8):
                T(lambda e, kc=kc: e.matmul(pj[:, 0:512], HT[:, kc, tt * 128:(tt + 1) * 128], WIN[:, kc, 512:1024],
                                            start=(kc == 0), stop=(kc == 7)), r=[RHT[tt], RW_GQA], w=[RPB[PJ]], sig=(kc == 7))
            yield
            stat, rs = new_stat()
            A(lambda e: e.activation(T1.f32(), pj[:, 0:384], AF.Square), r=[RPB[PJ]], w=T1.res)
            yield
            V(lambda e: e.tensor_reduce(out=stat[:, 0:6], in_=T1.f32().rearrange("p (h d) -> p h d", h=6),
                                        axis=AX.X, op=ALU.add), r=T1.res, w=[rs])
            yield
            rstd_from_sumsq(stat, rs, 64, cols=6)
            yield
            qk3 = QK.f32().rearrange("p (h d) -> p h d", h=6)
            V(lambda e: e.tensor_tensor(qk3, pj[:, 0:384].rearrange("p (h d) -> p h d", h=6),
                                        ap_bc_last(stat[:, 16:22], 64), ALU.mult), r=[RPB[PJ], rs], w=QK.res)
            yield
            A(lambda e: e.copy(va[:, :, koff + tt, 0:64], pj[:, 384:512].rearrange("p (v d) -> p v d", v=2)),
              r=[RPB[PJ]], w=bs.VA.res)
            ost = OST[tt % 2]
            if s == 0:
                A(lambda e: e.copy(ost.f32()[:, 0:128], pj[:, 384:512]), r=[RPB[PJ]], w=ost.res)
            yield
            V(lambda e: e.tensor_tensor(qk3, qk3, G6[:, l], ALU.mult), r=QK.res + PAR, w=QK.res)
            yield
            qkb = QKB[par]
            qdst = qkb.bf()[:, 0:256].rearrange("p (b a d) -> p a b d", a=2, b=2)
            kdst = qkb.bf()[:, 256:384]
            if s == 0:
                seq, tq = tt // 2, tt % 2
                D(lambda e: e.dma_start(out=ngk[seq, l, tq * 128:(tq + 1) * 128, :], in_=QK.f32()[:, 256:384]),
                  r=QK.res, out=True)
                D(lambda e: e.dma_start(out=ngv[seq, l, tq * 128:(tq + 1) * 128, :], in_=ost.f32()[:, 0:128]),
                  r=ost.res, out=True)
                V(lambda e: e.tensor_copy(qdst, QK.f32()[:, 0:256].rearrange("p (a b d) -> p a b d", a=2, b=2)),
                  r=QK.res, w=qkb.res)
                V(lambda e: e.tensor_copy(kdst, QK.f32()[:, 256:384]), r=QK.res, w=qkb.res)
                yield
            else:
                x5, xs5, c5, s5 = rope_views(QK.f32(), 6, tt)
                t15 = T1.f32().rearrange("p (h b f d) -> p h b f d", h=6, b=2, f=2)
                t25 = T2.f32().rearrange("p (h b f d) -> p h b f d", h=6, b=2, f=2)
                V(lambda e: e.tensor_tensor(t15, x5, c5, ALU.mult), r=QK.res + [R_ROPE], w=T1.res)
                yield
                V(lambda e: e.tensor_tensor(t25, xs5, s5, ALU.mult), r=QK.res + [R_ROPE], w=T2.res)
                yield
                G(lambda e: e.tensor_tensor(qdst, T1.f32()[:, 0:256].rearrange("p (a b d) -> p a b d", a=2, b=2),
                                            T2.f32()[:, 0:256].rearrange("p (a b d) -> p a b d", a=2, b=2), ALU.add),
                  r=T1.res + T2.res, w=qkb.res)
                G(lambda e: e.tensor_tensor(kdst, T1.f32()[:, 256:384], T2.f32()[:, 256:384], ALU.add),
                  r=T1.res + T2.res, w=qkb.res)
                yield
            ptr = bankbf(TRB)
            for j in range(3):
                T(lambda e, j=j: e.transpose(ptr[:, j * 128:(j + 1) * 128], qkb.bf()[:, j * 128:(j + 1) * 128], IDB[:]),
                  r=qkb.res + [R_ID], w=[RPB[TRB]], sig=(j == 2))
            yield
            if s == 0:
                dst3 = AP(qkall.tensor, qkall[:, tt * 128:tt * 128 + 1].offset, [list(qkall.ap[0]), [1024, 3], [1, 128]])
                A(lambda e: e.copy(dst3, ptr[:, 0:384].rearrange("p (u t) -> p u t", u=3)), r=[RPB[TRB]], w=bs.QK.res)
            else:
                A(lambda e: e.copy(qt2[:, :, tt * 128:(tt + 1) * 128], ptr[:, 0:256].rearrange("p (u t) -> p u t", u=2)),
                  r=[RPB[TRB]], w=bs.QK.res)
                A(lambda e: e.copy(kt[:, koff + tt, :], ptr[:, 256:384]), r=[RPB[TRB]], w=bs.QK.res)
            yield

        def phases(qb):
            lst = []
            for u in range(2):
                def head(u=u):
                    stat, rs = new_stat()
                    for m in range(2):
                        accv = bank(m)[:, 0:130].rearrange("p (q w) -> p q w", q=2)
                        V(lambda e, m=m, accv=accv: e.reciprocal(stat[:, 2 * m:2 * m + 2], accv[:, :, 64]), r=[RPB[m]], w=[rs])
                    for m in range(2):
                        hd = u + 2 * m
                        accv = bank(m)[:, 0:130].rearrange("p (q w) -> p q w", q=2)
                        obv = OBT.bf().rearrange("p (q h d) -> p q h d", q=2, h=4)[:, :, hd, :]
                        V(lambda e, m=m, accv=accv, obv=obv: e.tensor_tensor(
                            obv, accv[:, :, 0:64], ap_bc_last(stat[:, 2 * m:2 * m + 2], 64), ALU.mult),
                          r=[RPB[m], rs], w=OBT.res)
                lst.append((attn_core(s, qb, qt2[:, u, :], kt, lambda m: va[:, m], 64, allres), head))
            return lst

        def tail(qb):
            pso = bankbf(2)[:, 512:1024]
            for qt in range(2):
                for pair in range(2):
                    T(lambda e, qt=qt, pair=pair: e.transpose(pso[:, (pair * 2 + qt) * 128:(pair * 2 + qt + 1) * 128],
                                                              OBT.bf()[:, qt * 256 + pair * 128:qt * 256 + (pair + 1) * 128], IDB[:]),
                      r=OBT.res + [R_ID], w=[RPB[2]], sig=(qt == 1 and pair == 1))
            yield
            for pair in range(2):
                V(lambda e, pair=pair: e.tensor_tensor(MIXT[:, 2 + pair, qb * 256:(qb + 1) * 256], pso[:, pair * 256:(pair + 1) * 256],
                                                       MIXT[:, 2 + pair, qb * 256:(qb + 1) * 256], ALU.mult),
                  r=[RPB[2], RMIX[2 + pair][2 * qb], RMIX[2 + pair][2 * qb + 1]],
                  w=[RMIX[2 + pair][2 * qb], RMIX[2 + pair][2 * qb + 1]])
            yield

        return prep_head, prep_tile, phases, tail

    def diff_unit(s, l, h, bs):
        NPAR = 4 if s == 0 else 2
        koff = 0 if s == 0 else 4
        qkall = bs.QK.bf()
        qt = qkall[:, 0:1024]
        kt = qkall[:, 1024:1024 + 1536].rearrange("p (k t) -> p k t", t=128)
        va = bs.VA.bf()[:, 0:12 * 130].rearrange("p (k w) -> p k w", k=12)
        allres = bs.QK.res + bs.VA.res
        rhs_all = WIN[:, :, C_DQ:C_DQ + 1536].rearrange("p k (s c) -> p k s c", s=3)[:, :, :, h * 128:(h + 1) * 128]

        def prep_head():
            G(lambda e: e.memset(va[:, :, 128:129], 1.0), w=bs.VA.res)
            yield
            if s == 1:
                DG(lambda e: e.dma_start(out=va[:, 0:4, 0:128],
                                         in_=cdv[l].rearrange("(c p) d -> p c d", p=128)[:, :, h * 128:(h + 1) * 128]), w=bs.VA.res)
                ptr = cache_k_load(cdk[l].rearrange("(c p) d -> p c d", p=128)[:, :, h * 128:(h + 1) * 128])
                yield
                A(lambda e: e.copy(qkall[:, 1024:1024 + 512], ptr[:, 0:512]), r=[RPB[PJ_B]], w=bs.QK.res)
                yield

        def prep_tile(tt):
            par = tt % NPAR
            PJ = (6, 7, 4, 5)[par]
            pj = bank(PJ)
            t1 = T1S[par % 2]; t2 = T2S[par % 2]
            for kc in range(8):
                T(lambda e, kc=kc: e.matmul(pj[:, 0:384], HT[:, kc, tt * 128:(tt + 1) * 128], rhs_all[:, kc],
                                            start=(kc == 0), stop=(kc == 7)), r=[RHT[tt], RW_DIFF[h]], w=[RPB[PJ]], sig=(kc == 7))
            yield
            qkb = QKBD[par]
            if s == 0:
                seq, tq = tt // 2, tt % 2
                ost = OSTD[par]
                A(lambda e: e.copy(ost.f32(), pj[:, 128:384]), r=[RPB[PJ]], w=ost.res)
                yield
                V(lambda e: e.tensor_copy(qkb.bf()[:, 0:256], pj[:, 0:256]), r=[RPB[PJ]], w=qkb.res)
                yield
                D(lambda e: e.dma_start(out=ndk[seq, l, tq * 128:(tq + 1) * 128, h * 128:(h + 1) * 128],
                                        in_=ost.f32()[:, 0:128]), r=ost.res, out=True)
                D(lambda e: e.dma_start(out=ndv[seq, l, tq * 128:(tq + 1) * 128, h * 128:(h + 1) * 128],
                                        in_=ost.f32()[:, 128:256]), r=ost.res, out=True)
            else:
                x5, xs5, c5, s5 = rope_views(pj[:, 0:256], 4, tt)
                t15 = t1.f32().rearrange("p (h b f d) -> p h b f d", h=4, b=2, f=2)
                t25 = t2.f32().rearrange("p (h b f d) -> p h b f d", h=4, b=2, f=2)
                V(lambda e: e.tensor_tensor(t15, x5, c5, ALU.mult), r=[RPB[PJ], R_ROPE], w=t1.res)
                yield
                V(lambda e: e.tensor_tensor(t25, xs5, s5, ALU.mult), r=[RPB[PJ], R_ROPE], w=t2.res)
                yield
                V(lambda e: e.tensor_tensor(qkb.bf()[:, 0:256], t1.f32(), t2.f32(), ALU.add), r=t1.res + t2.res, w=qkb.res)
                yield
            A(lambda e: e.copy(va[:, koff + tt, 0:128], pj[:, 256:384]), r=[RPB[PJ]], w=bs.VA.res)
            yield
            ptr = bankbf(PJ)[:, 768:1024]
            for j in range(2):
                T(lambda e, j=j: e.transpose(ptr[:, j * 128:(j + 1) * 128], qkb.bf()[:, j * 128:(j + 1) * 128], IDB[:]),
                  r=qkb.res + [R_ID], w=[RPB[PJ]], sig=(j == 1))
            yield
            dst2 = AP(qkall.tensor, qkall[:, tt * 128:tt * 128 + 1].offset, [list(qkall.ap[0]), [1024 + koff * 128, 2], [1, 128]])
            A(lambda e: e.copy(dst2, ptr[:, 0:256].rearrange("p (u t) -> p u t", u=2)), r=[RPB[PJ]], w=bs.QK.res)
            yield

        fin = {}

        def phases(qb):
            def head():
                acc0 = bank(0)[:, 0:258].rearrange("p (q w) -> p q w", q=2)
                acc1 = bank(1)[:, 0:258].rearrange("p (q w) -> p q w", q=2)
                stat, rs = new_stat()
                V(lambda e: e.reciprocal(stat[:, 0:2], acc0[:, :, 128]), r=[RPB[0]], w=[rs])
                V(lambda e: e.reciprocal(stat[:, 2:4], acc1[:, :, 128]), r=[RPB[1]], w=[rs])
                V(lambda e: e.tensor_scalar(out=stat[:, 4:6], in0=stat[:, 2:4], scalar1=NLAM[:, l:l + 1], scalar2=None, op0=ALU.mult),
                  r=[rs] + PAR, w=[rs])
                o0 = O0T.f32().rearrange("p (q d) -> p q d", q=2); do = DOT.f32().rearrange("p (q d) -> p q d", q=2)
                V(lambda e: e.tensor_tensor(o0, acc0[:, :, 0:128], ap_bc_last(stat[:, 0:2], 128), ALU.mult), r=[RPB[0], rs], w=O0T.res)
                V(lambda e: e.tensor_tensor(do, acc1[:, :, 0:128], ap_bc_last(stat[:, 4:6], 128), ALU.mult), r=[RPB[1], rs], w=DOT.res)
            return [(attn_core(s, qb, qt, kt, lambda m: va, 128, allres), head)]

        def tail(qb):
            o0 = O0T.f32().rearrange("p (q d) -> p q d", q=2); do = DOT.f32().rearrange("p (q d) -> p q d", q=2)
            V(lambda e: e.tensor_tensor(do, do, o0, ALU.add), r=DOT.res + O0T.res, w=DOT.res)
            yield
            stat2, rs2 = new_stat()
            for qi in range(2):
                A(lambda e, qi=qi: e.activation(O0T.f32()[:, qi * 128:(qi + 1) * 128], DOT.f32()[:, qi * 128:(qi + 1) * 128],
                                                AF.Square, accum_out=stat2[:, qi:qi + 1]), r=DOT.res, w=O0T.res + [rs2])
            yield
            rstd_from_sumsq(stat2, rs2, 128, cols=2)
            yield
            dob = DOBT.bf().rearrange("p (q d) -> p q d", q=2)
            V(lambda e: e.tensor_tensor(dob, do, ap_bc_last(stat2[:, 16:18], 128), ALU.mult), r=DOT.res + [rs2], w=DOBT.res)
            yield
            pso = bankbf(2)[:, 512:768]
            for qi in range(2):
                T(lambda e, qi=qi: e.transpose(pso[:, qi * 128:(qi + 1) * 128], DOBT.bf()[:, qi * 128:(qi + 1) * 128], IDB[:]),
                  r=DOBT.res + [R_ID], w=[RPB[2]], sig=(qi == 1))
            yield
            V(lambda e: e.scalar_tensor_tensor(out=MIXT[:, 4 + h, qb * 256:(qb + 1) * 256], in0=pso[:, 0:256], scalar=GSUBS[:, l:l + 1],
                                               in1=MIXT[:, 4 + h, qb * 256:(qb + 1) * 256], op0=ALU.mult, op1=ALU.mult),
              r=[RPB[2], RMIX[4 + h][2 * qb], RMIX[4 + h][2 * qb + 1]] + PAR, w=[RMIX[4 + h][2 * qb], RMIX[4 + h][2 * qb + 1]])
            yield

        return prep_head, prep_tile, phases, tail

    def attention(s, l, nxt, lru_gen):
        sets = [BS_A, BS_B]
        units = [gqa_unit(s, l, sets[0])] + [diff_unit(s, l, h, sets[(h + 1) % 2]) for h in range(4)]
        loads = [lambda: load_win_gqa(nxt)] + [lambda h=h: load_win_diff(nxt, h) for h in range(4)]

        def prep_gen(u, tiles=range(8), head=True):
            if head:
                yield from u[0]()
            for tt in tiles:
                yield from u[1](tt)

        def core_gen(u):
            pend = [None]
            kstep = 3 if s == 0 else 1

            def step_pending(n):
                for _ in range(n):
                    if pend[0] is None:
                        return
                    try:
                        next(pend[0])
                    except StopIteration:
                        pend[0] = None

            for qb in range(4):
                for attn_gen, head in u[2](qb):
                    for _ in attn_gen:
                        step_pending(kstep)
                        yield
                    step_pending(1000)
                    head()
                    yield
                pend[0] = u[3](qb)
            while pend[0] is not None:
                step_pending(1)
                yield

        for _ in units[0][0]():
            pass
        interleave([prep_gen(units[0], range(0, 8, 2), False), prep_gen(units[0], range(1, 8, 2), False), lru_gen])
        if nxt is not None:
            load_win_lru(nxt)
            loads[0]()
        for i, u in enumerate(units):
            nx = units[i + 1] if i + 1 < len(units) else None
            if nx is None:
                interleave([core_gen(u)])
            else:
                npar = 4 if s == 0 else 2
                interleave([core_gen(u)] + [prep_gen(nx, range(p_, 8, npar), p_ == 0) for p_ in range(npar)])
            if nx is not None and nxt is not None:
                loads[i + 1]()

    def out_proj(s, l, nstep):
        c = s
        psg = dbank(2)
        for kc in range(8):
            bc = BCT[kc % 2]
            V(lambda e, kc=kc, bc=bc: e.tensor_copy(bc.f32(), ap_bc_last(GPT[:, l, kc, c], 128)), r=[R_MOD[l]], w=bc.res)
            T(lambda e, kc=kc, bc=bc: e.matmul(psg[:, kc * 128:(kc + 1) * 128], bc.f32(), IDF[:], start=True, stop=True),
              r=bc.res + [R_ID], w=[RPB[4 + kc // 4]])
        A(lambda e: e.copy(GPB.f32(), psg), r=rdb(2), w=GPB.res)

        def mm(tt):
            d = tt % 2
            psy = dbank(d)
            for half in range(2):
                for fc in range(8):
                    T(lambda e, half=half, fc=fc: e.matmul(
                        psy[:, half * 512:(half + 1) * 512], MIXT[:, fc, tt * 128:(tt + 1) * 128], WOUT[:, fc, half * 512:(half + 1) * 512],
                        start=(fc == 0), stop=(fc == 7)), r=[RMIX[fc][tt], RW_OUT], w=[RPB[2 * d + half]], sig=(fc == 7))

        def post(tt):
            d = tt % 2
            psy = dbank(d)
            stat, rs = new_stat()
            A(lambda e: e.activation(JUNK2.bf(), psy, AF.Square, accum_out=stat[:, 0:1]), r=rdb(d), w=JUNK2.res + [rs])
            yield
            V(lambda e: e.tensor_tensor(TMP.f32(), psy, GPB.f32(), ALU.mult), r=rdb(d) + GPB.res, w=TMP.res)
            yield
            if tt + 2 < 8:
                mm(tt + 2)
            rstd_from_sumsq(stat, rs, 1024)
            yield
            V(lambda e: e.scalar_tensor_tensor(out=X[:, tt, :], in0=TMP.f32(), scalar=stat[:, 16:17], in1=X[:, tt, :],
                                               op0=ALU.mult, op1=ALU.add), r=TMP.res + [rs, RX[tt]], w=[RX[tt]])
            yield
            if l == 1:
                D(lambda e: e.dma_start(out=yout[s][tt * 128:(tt + 1) * 128, :], in_=X[:, tt, :]), r=[RX[tt]], out=True)
            yield

        def pn(tt):
            if nstep is None or tt < 0:
                return
            if nstep[1] == 0:
                x_load(nstep[0], tt)
            yield from prenorm_tile(nstep[0], nstep[1], tt)

        mm(0)
        mm(1)
        for tt in range(8):
            interleave([post(tt), pn(tt - 1)])
        interleave([pn(7)])

    prenorm(*steps[0])
    for si, (s, l) in enumerate(steps):
        nstep = steps[si + 1] if si + 1 < len(steps) else None
        nxt = nstep[1] if nstep is not None else None
        def gates_then_lru(s=s, l=l, si=si):
            g = mod(1, [Tile(U.off, 2048), Tile(II.off, 2048)], 0) if si == 0 else None
            if g is not None:
                next(g)
            yield from all_gates(s, l)
            if g is not None:
                yield from g
            yield from lru(s, l)

        attention(s, l, nxt, gates_then_lru())
        out_proj(s, l, nstep)
        if nxt is not None:
            load_wout(nxt)

    P.emit(nc, st)
    st.close()
    return nc


def _rope_tables():
    inv = np.power(np.float32(10000.0), -np.arange(16, dtype=np.float32) / np.float32(16)).astype(np.float32)
    t = np.arange(1024)
    row = (t // 64).astype(np.float32)
    col = (t % 64).astype(np.float32)
    ar = (row[:, None] * inv[None]).astype(np.float32)
    ac = (col[:, None] * inv[None]).astype(np.float32)
    C = np.concatenate([np.cos(ar), np.cos(ar), np.cos(ac), np.cos(ac)], axis=1).astype(np.float32)
    S = np.concatenate([-np.sin(ar), np.sin(ar), -np.sin(ac), np.sin(ac)], axis=1).astype(np.float32)
    return np.ascontiguousarray(C), np.ascontiguousarray(S)


_NC_CACHE = {}


def kernel(x_prompt, x_sample, cache_gqa_k, cache_gqa_v, cache_diff_k, cache_diff_v, state_lru,
           c, c_ctx, w_mod, b_mod, g_pre, g_post, w_in, w_out, lru_conv_w, lru_conv_b,
           lru_wa, lru_ba, lru_wx, lru_bx, lru_lambda, gqa_gq, gqa_gk, diff_lam, diff_gsub):
    f = lambda a: np.ascontiguousarray(np.asarray(a, dtype=np.float32))
    x_prompt, x_sample = f(x_prompt), f(x_sample)
    ropeC, ropeS = _rope_tables()
    shared = {
        "w_mod": f(w_mod), "b_mod": f(b_mod), "g_pre": f(g_pre), "g_post": f(g_post),
        "w_in": f(w_in), "w_out": f(w_out), "conv_w": f(lru_conv_w), "conv_b": f(lru_conv_b),
        "lru_wa": f(lru_wa), "lru_ba": f(lru_ba), "lru_wx": f(lru_wx), "lru_bx": f(lru_bx),
        "lru_lambda": f(lru_lambda), "gqa_gq": f(gqa_gq), "gqa_gk": f(gqa_gk),
        "diff_lam": f(diff_lam), "diff_gsub": f(diff_gsub),
        "ident": np.eye(128, dtype=np.float32), "ropec": ropeC, "ropes": ropeS,
    }
    c = f(c); c_ctx = f(c_ctx)
    fm = lambda a: np.moveaxis(a.reshape(a.shape[:-1] + (a.shape[-1] // 128, 128)), -1, 0)
    b_mod_, g_pre_, g_post_ = shared["b_mod"], shared["g_pre"], shared["g_post"]
    cw_ = fm(shared["conv_w"])
    common = [fm(b_mod_).reshape(128, -1), fm(g_pre_).reshape(128, -1), fm(g_post_).reshape(128, -1),
              np.transpose(cw_, (0, 1, 3, 2)).reshape(128, -1), fm(shared["conv_b"]).reshape(128, -1),
              fm(shared["lru_ba"]).reshape(128, -1), fm(shared["lru_bx"]).reshape(128, -1),
              fm(shared["lru_lambda"]).reshape(128, -1)]
    gsub_ = np.ascontiguousarray(shared["diff_gsub"].T)
    wbd = np.zeros((128, 2, 8, 128), np.float32)
    for typ, wsrc in enumerate((shared["lru_wa"], shared["lru_wx"])):
        for l_ in range(2):
            for d_ in range(2):
                for n_ in range(4):
                    ch_, b_ = n_ // 2, n_ % 2
                    wbd[b_ * 64:(b_ + 1) * 64, l_, d_ * 4 + typ * 2 + ch_, b_ * 64:(b_ + 1) * 64] = wsrc[l_, d_, n_]
    shared["wbd"] = wbd.reshape(128, -1)
    for k_ in ("b_mod", "g_pre", "g_post", "conv_w", "conv_b", "lru_wa", "lru_ba", "lru_wx", "lru_bx", "lru_lambda", "diff_gsub"):
        del shared[k_]
    in_maps = []
    for i in range(N_CORES):
        m = dict(shared)
        cond_i = np.stack([c_ctx, c[i]], axis=0)
        condT = np.transpose(fm(cond_i), (0, 2, 1)).reshape(128, -1)
        h0_ = fm(f(state_lru[i])).reshape(128, -1)
        m["pvec"] = np.ascontiguousarray(np.concatenate([condT] + common + [h0_, gsub_], axis=1), dtype=np.float32)
        assert m["pvec"].shape == (128, NPV)
        m["xp"] = x_prompt[4 * i:4 * i + 4].reshape(1024, 1024)
        m["xs"] = x_sample[i]
        m["cgk"] = f(cache_gqa_k[i]).reshape(2, 512, 128)
        m["cgv"] = f(cache_gqa_v[i]).reshape(2, 512, 128)
        m["cdk"] = f(cache_diff_k[i]).reshape(2, 512, 512)
        m["cdv"] = f(cache_diff_v[i]).reshape(2, 512, 512)
        in_maps.append(m)
    if "nc" not in _NC_CACHE:
        _NC_CACHE["nc"] = build()
    nc = _NC_CACHE["nc"]
    ncore = int(os.environ.get("KCORES", str(N_CORES)))
    res = run_bass_kernel_spmd(nc, in_maps[:ncore], core_ids=list(range(ncore)))
    R = list(res.results) + [res.results[0]] * (N_CORES - ncore)
    y_prompt = np.concatenate([R[i]["yp"].reshape(4, 256, 1024) for i in range(N_CORES)], axis=0)
    y_sample = np.stack([R[i]["ys"] for i in range(N_CORES)], axis=0)
    new_gqa_k = np.concatenate([R[i]["ngk"].reshape(4, 2, 256, 2, 64) for i in range(N_CORES)], axis=0)
    new_gqa_v = np.concatenate([R[i]["ngv"].reshape(4, 2, 256, 2, 64) for i in range(N_CORES)], axis=0)
    new_diff_k = np.concatenate([R[i]["ndk"].reshape(4, 2, 256, 4, 2, 64) for i in range(N_CORES)], axis=0)
    new_diff_v = np.concatenate([R[i]["ndv"].reshape(4, 2, 256, 4, 128) for i in range(N_CORES)], axis=0)
    new_lru = np.concatenate([R[i]["nlru"] for i in range(N_CORES)], axis=0)
    return (y_prompt.astype(np.float32), y_sample.astype(np.float32), new_gqa_k.astype(np.float32),
            new_gqa_v.astype(np.float32), new_diff_k.astype(np.float32), new_diff_v.astype(np.float32),
            new_lru.astype(np.float32))
```

```python
import math
import os
from contextlib import ExitStack

import numpy as np
import concourse.bass as bass
import concourse.mybir as mybir
from concourse.alu_op_type import AluOpType as ALU
from concourse.ap import AP
from concourse.bass_utils import run_bass_kernel_spmd

F32 = mybir.dt.float32
BF16 = mybir.dt.bfloat16
AF = mybir.ActivationFunctionType
AX = mybir.AxisListType

EPS = 1e-6
NPV = 150
N_CORES = 8
ENGS = ["sync", "scalar", "gpsimd", "vector", "tensor"]
N_DMA_SEMS = 64
N_HW_SEMS = 40
SAME_ENGINE_SYNC = True


class Res:
    __slots__ = ("w", "r", "x")

    def __init__(self, x=False):
        self.w = None
        self.r = {}
        self.x = x


class Prog:
    def __init__(self):
        self.q = {e: [] for e in ENGS}
        self.cnt = {e: 0 for e in ENGS}
        self.dma_val = [0] * N_DMA_SEMS
        self.dma_next = {}
        self.waited = {e: {} for e in ENGS}
        self.out_tokens = []
        self.last_sig = {e: True for e in ENGS}

    def op(self, eng, fn, reads=(), writes=(), dma=False, is_output=False, signal=True):
        deps = {}
        same = SAME_ENGINE_SYNC and eng != "tensor"

        def add(tok, same_ok):
            if tok is None:
                return
            k, v = tok
            if k == ("c", eng) and not same_ok:
                return
            if deps.get(k, 0) < v:
                deps[k] = v

        for r in reads:
            add(r.w, same)
            if r.x:
                for k, v in r.r.items():
                    add((k, v), False)
        for w in writes:
            add(w.w, same)
            for k, v in w.r.items():
                add((k, v), same)
        if dma:
            lo, hi = (0, N_HW_SEMS) if eng == "sync" else (N_HW_SEMS, N_DMA_SEMS)
            si = self.dma_next.get(eng, lo)
            self.dma_next[eng] = lo + (si + 1 - lo) % (hi - lo)
            if self.dma_val[si] > 0:
                k = ("d", si)
                if deps.get(k, 0) < self.dma_val[si]:
                    deps[k] = self.dma_val[si]
            self.dma_val[si] += 16
            tok = (("d", si), self.dma_val[si])
        elif signal:
            self.cnt[eng] += 1
            tok = (("c", eng), self.cnt[eng])
        else:
            tok = (("c", eng), self.cnt[eng] + 1)
        self.last_sig[eng] = signal or dma
        waits = []
        wd = self.waited[eng]
        for k, v in deps.items():
            if wd.get(k, 0) >= v:
                continue
            wd[k] = v
            waits.append((k, v))
        k, v = tok
        for r in reads:
            if r.r.get(k, 0) < v:
                r.r[k] = v
        for w in writes:
            w.w = tok
            w.r = {}
        self.q[eng].append((waits, fn, tok, signal))
        if is_output:
            self.out_tokens.append(tok)
        return tok

    def emit(self, nc, st):
        csem = {e: st.enter_context(nc.semaphore("c_" + e)) for e in ENGS}
        dsem = [st.enter_context(nc.semaphore("d_%d" % i)) for i in range(N_DMA_SEMS)]
        block = st.enter_context(nc.Block())

        def sem_of(k):
            return csem[k[1]] if k[0] == "c" else dsem[k[1]]

        def run(engname, engine):
            assert self.last_sig[engname], engname
            for waits, fn, tok, signal in self.q[engname]:
                for k, v in waits:
                    engine.wait_ge(sem_of(k), v)
                ins = fn(engine)
                k, v = tok
                if k[0] == "c":
                    if signal:
                        ins.then_inc(csem[k[1]], 1)
                else:
                    ins.then_inc(dsem[k[1]], 16)
            if engname == "sync":
                final = {}
                for k, v in self.out_tokens:
                    if final.get(k, 0) < v:
                        final[k] = v
                for k, v in final.items():
                    engine.wait_ge(sem_of(k), v)

        @block.sync
        def _(e):
            run("sync", e)

        @block.scalar
        def _(e):
            run("scalar", e)

        @block.gpsimd
        def _(e):
            run("gpsimd", e)

        @block.vector
        def _(e):
            run("vector", e)

        @block.tensor
        def _(e):
            run("tensor", e)


def ap_bc_mid(ap, n):
    a = [list(x) for x in ap.ap]
    return AP(ap.tensor, ap.offset, [a[0], [0, n]] + a[1:])


def ap_bc_last(ap, n):
    a = [list(x) for x in ap.ap]
    return AP(ap.tensor, ap.offset, a + [[0, n]])


def lambda_init(layer):
    return 0.8 - 0.6 * math.exp(-0.3 * layer)


C_LRUX, C_LRUG, C_GQ, C_GK, C_GV, C_GG, C_DQ, C_DK, C_DV, C_DG = (
    0, 256, 512, 768, 896, 1024, 1280, 1792, 2304, 2816)


def build():
    nc = bass.Bass("TRN2", target_bir_lowering=False)
    P = Prog()
    st = ExitStack()

    def din(name, shape):
        return nc.dram_tensor(name, list(shape), F32, kind="ExternalInput").ap()

    def dout(name, shape):
        return nc.dram_tensor(name, list(shape), F32, kind="ExternalOutput").ap()

    xin = [din("xp", [1024, 1024]), din("xs", [1024, 1024])]
    cgk = din("cgk", [2, 512, 128]); cgv = din("cgv", [2, 512, 128])
    cdk = din("cdk", [2, 512, 512]); cdv = din("cdv", [2, 512, 512])

    w_mod = din("w_mod", [2, 1024, 3072])
    w_in = din("w_in", [2, 1024, 3328]); w_out = din("w_out", [2, 1024, 1024])
    gq = din("gqa_gq", [2, 64]); gk = din("gqa_gk", [2, 64])
    dlam = din("diff_lam", [2, 4, 64])
    ident = din("ident", [128, 128]); ropec = din("ropec", [1024, 64]); ropes = din("ropes", [1024, 64])
    pvec = din("pvec", [128, NPV]); wbd_in = din("wbd", [128, 2 * 8 * 128])

    yout = [dout("yp", [1024, 1024]), dout("ys", [1024, 1024])]
    ngk = dout("ngk", [4, 2, 256, 128]); ngv = dout("ngv", [4, 2, 256, 128])
    ndk = dout("ndk", [4, 2, 256, 512]); ndv = dout("ndv", [4, 2, 256, 512])
    nlru = dout("nlru", [4, 2, 2, 256])

    def sb(name, shape, dt):
        return st.enter_context(nc.sbuf_tensor(name, list(shape), dt))

    X = sb("X", [128, 8, 1024], F32); RX = [Res() for _ in range(8)]
    WIN = sb("WIN", [128, 8, 3328], BF16)
    RW_LRU = Res(); RW_GQA = Res(); RW_DIFF = [Res() for _ in range(4)]
    WOUT = sb("WOUT", [128, 8, 1024], BF16); RW_OUT = Res()
    HT = sb("HT", [128, 8, 1024], BF16); RHT = [Res() for _ in range(8)]
    MIXT = sb("MIXT", [128, 8, 1024], BF16); RMIX = [[Res() for _ in range(8)] for _ in range(8)]

    IDF = sb("IDF", [128, 128], F32); IDB = sb("IDB", [128, 128], BF16); R_ID = Res()
    ROPEC = sb("ROPEC", [128, 8, 64], F32); ROPES = sb("ROPES", [128, 8, 64], F32); R_ROPE = Res()
    PVEC = sb("PVEC", [128, NPV], F32)
    _o = [0]

    def pv(n, pattern=None, **kw):
        v = PVEC[:, _o[0]:_o[0] + n]
        _o[0] += n
        return v.rearrange(pattern, **kw) if pattern else v
    CONDT = pv(16, "p (k c) -> p k c", c=2)
    BMT = pv(48, "p (l k) -> p l k", l=2)
    GPRE = pv(16, "p (l k) -> p l k", l=2)
    GPOST = pv(16, "p (l k) -> p l k", l=2)
    CW = pv(16, "p (l c j) -> p l c j", l=2, c=2)
    CB = pv(4, "p (l c) -> p l c", l=2)
    BA = pv(8, "p (l d c) -> p l d c", l=2, d=2)
    BXX = pv(8, "p (l d c) -> p l d c", l=2, d=2)
    LAMT = pv(8, "p (l d c) -> p l d c", l=2, d=2)
    H0 = pv(8, "p (l d c) -> p l d c", l=2, d=2)
    GSUBS = pv(2)
    assert _o[0] == NPV
    SC = sb("SC", [128, 8, 2], BF16); R_SC = Res()
    MODT = sb("MODT", [128, 2, 24, 2], F32); GS = sb("GS", [128, 2, 8, 2], F32); GPT = sb("GPT", [128, 2, 8, 2], F32)
    R_MOD = [Res(), Res()]
    CEXP = sb("CEXP", [128, 2, 2, 2], F32)
    WBD = sb("WBD", [128, 2, 8, 128], BF16)
    G6 = sb("G6", [128, 2, 6, 64], F32)
    DLS = sb("DLS", [128, 2, 4], F32)
    NLAM = sb("NLAM", [128, 2], F32)
    PAR = []

    def pres():
        r = Res()
        PAR.append(r)
        return r
    STP = sb("STP", [128, 32, 24], F32); RST = [Res() for _ in range(32)]
    st_i = [0]

    def new_stat():
        i = st_i[0] % 32
        st_i[0] += 1
        return STP[:, i, :], RST[i]

    GR = 64
    A_COLS = 15040
    ARENA = sb("ARENA", [128, A_COLS], F32)
    RGR = [Res() for _ in range(A_COLS // GR)]
    bump = [0]

    class Tile:
        def __init__(self, off, ncols):
            assert off % GR == 0
            self.off = off
            self.ncols = ncols
            self.res = RGR[off // GR:(off + ncols + GR - 1) // GR]

        def f32(self):
            return ARENA[:, self.off:self.off + self.ncols]

        def bf(self):
            return ARENA[:, self.off:self.off + self.ncols].bitcast(BF16)

    def alloc(ncols):
        off = bump[0]
        n = ((ncols + GR - 1) // GR) * GR
        bump[0] += n
        assert bump[0] <= A_COLS, bump[0]
        return Tile(off, ncols)

    XPAD = alloc(1088); U = alloc(1024); RR = alloc(1024); II = alloc(1024); TT = alloc(1024)
    Y0 = alloc(1024); Y1 = alloc(1024); UBF = alloc(512)
    XN = U; JUNK = Tile(RR.off, 512)
    GPB = II; TMP = TT; JUNK2 = Tile(Y0.off, 512); BCT = [Tile(Y1.off, 128), Tile(Y1.off + 128, 128)]
    QT = alloc(1024)
    KT = alloc(768)
    VA = alloc(1024)
    GATE = alloc(1024)
    assert KT.off == QT.off + 1024 and VA.off == QT.off + 1792 and GATE.off == QT.off + 2816
    PTB = [alloc(256) for _ in range(3)]
    QK = alloc(384); T1 = alloc(384)
    QKB = [alloc(192), alloc(192)]
    T2 = alloc(384)
    CKB = alloc(256)
    assert CKB.off == T2.off + 384
    OST = [Tile(T2.off, 256), Tile(T2.off + 256, 256)]
    O0T = alloc(256); DOT = alloc(256)
    DOBT = alloc(128)
    OBT = alloc(256)
    assert T1.off == QK.off + 384 and CKB.off == T2.off + 384
    assert DOT.off == O0T.off + 256 and DOBT.off == O0T.off + 512
    GQT = [(QK, T1, T2), (Tile(GATE.off, 384), Tile(GATE.off + 384, 384), Tile(O0T.off, 384))]
    T1S = [Tile(QK.off, 256), Tile(QK.off + 256, 256)]
    T2S = [Tile(T2.off, 256), Tile(T2.off + 256, 256)]
    WM = [Tile(QT.off, 2048), Tile(QT.off + 2048, 2048), Tile(QT.off + 4096, 2048)]
    WBDF_T = Tile(U.off, 2048); DLB_T = Tile(II.off, 512); DLP_T = Tile(II.off + 512, 256)
    GQB_T = Tile(II.off + 768, 128); GKB_T = Tile(II.off + 896, 128)
    WBDF = WBDF_T.f32().rearrange("p (l i c) -> p l i c", l=2, i=8)
    DLB = DLB_T.f32().rearrange("p (l r d) -> p l r d", l=2, r=4)
    DLP = DLP_T.f32().rearrange("p (l r d) -> p l r d", l=2, r=2)
    GQB = GQB_T.f32().rearrange("p (l d) -> p l d", l=2)
    GKB = GKB_T.f32().rearrange("p (l d) -> p l d", l=2)

    PS = st.enter_context(nc.psum_tensor("PS", [128, 4096], F32))
    RPB = [Res(x=True) for _ in range(8)]

    def bank(b):
        return PS[:, b * 512:(b + 1) * 512]

    def bankbf(b):
        return PS[:, b * 512:(b + 1) * 512].bitcast(BF16)

    def dbank(d):
        return PS[:, d * 1024:(d + 1) * 1024]

    def rdb(d):
        return [RPB[2 * d], RPB[2 * d + 1]]

    def V(fn, r=(), w=()):
        return P.op("vector", fn, r, w)

    def A(fn, r=(), w=()):
        return P.op("scalar", fn, r, w)

    def G(fn, r=(), w=()):
        return P.op("gpsimd", fn, r, w)

    def T(fn, r=(), w=(), sig=True):
        return P.op("tensor", fn, r, w, signal=sig)

    def D(fn, r=(), w=(), out=False):
        return P.op("sync", fn, r, w, dma=True, is_output=out)

    def DG(fn, r=(), w=()):
        return P.op("gpsimd", fn, r, w, dma=True)

    def load_win_lru(l):
        src = w_in[l].rearrange("(kc p) c -> p kc c", p=128)
        DG(lambda e: e.dma_start(out=WIN[:, :, 0:512], in_=src[:, :, 0:512]), w=[RW_LRU])

    def load_win_gqa(l):
        src = w_in[l].rearrange("(kc p) c -> p kc c", p=128)
        DG(lambda e: e.dma_start(out=WIN[:, :, 512:1280], in_=src[:, :, 512:1280]), w=[RW_GQA])

    def load_win_diff(l, h):
        src = w_in[l].rearrange("(kc p) c -> p kc c", p=128)
        for seg in range(4):
            c0 = C_DQ + seg * 512 + h * 128
            DG(lambda e, c0=c0: e.dma_start(out=WIN[:, :, c0:c0 + 128], in_=src[:, :, c0:c0 + 128]), w=[RW_DIFF[h]])

    def load_wout(l):
        src = w_out[l].rearrange("(kc p) c -> p kc c", p=128)
        DG(lambda e: e.dma_start(out=WOUT[:], in_=src), w=[RW_OUT])

    r_pv = pres()
    D(lambda e: e.dma_start(out=PVEC[:], in_=pvec), w=[r_pv])
    D(lambda e: e.dma_start(out=IDF[:], in_=ident), w=[R_ID])
    DG(lambda e: e.dma_start(out=IDB[:], in_=ident), w=[R_ID])
    A(lambda e: e.activation(SC[:], CONDT, AF.Silu), r=[r_pv], w=[R_SC])
    for tt in range(8):
        D(lambda e, tt=tt: e.dma_start(out=X[:, tt, :], in_=xin[0][tt * 128:(tt + 1) * 128, :]), w=[RX[tt]])
    D(lambda e: e.dma_start(out=ROPEC[:], in_=ropec.rearrange("(t p) d -> p t d", p=128)), w=[R_ROPE])
    D(lambda e: e.dma_start(out=ROPES[:], in_=ropes.rearrange("(t p) d -> p t d", p=128)), w=[R_ROPE])

    def small_load(out_ap, in_ap, extra=()):
        r = pres()
        D(lambda e: e.dma_start(out=out_ap, in_=in_ap, allow_slow_non_contiguous=True), w=[r] + list(extra))
        return r

    r_gqb = []; r_gkb = []; r_dlb = []
    for l in range(2):
        r_gqb.append(small_load(GQB[:, l, :], gq[l].partition_broadcast(128), GQB_T.res))
        r_gkb.append(small_load(GKB[:, l, :], gk[l].partition_broadcast(128), GKB_T.res))
        r_dlb.append(small_load(DLB[:, l, :, :], dlam[l].partition_broadcast(128), DLB_T.res))
    r_wbd = pres()
    DG(lambda e: e.dma_start(out=WBD[:].rearrange("p l i c -> p (l i c)"), in_=wbd_in), w=[r_wbd])
    r_cexp = pres()
    A(lambda e: e.activation(CEXP[:], LAMT, AF.Exp, scale=-1.0), r=[r_pv], w=[r_cexp])
    A(lambda e: e.activation(CEXP[:], CEXP[:], AF.Ln, scale=1.0, bias=1.0), r=[r_cexp], w=[r_cexp])
    V(lambda e: e.tensor_scalar(out=CEXP[:], in0=CEXP[:], scalar1=-8.0, scalar2=None, op0=ALU.mult), r=[r_cexp], w=[r_cexp])
    r_g6 = pres()
    for l in range(2):
        V(lambda e, l=l: e.tensor_copy(G6[:, l, 0:4, :], ap_bc_mid(GQB[:, l, :], 4)), r=r_gqb + GQB_T.res, w=[r_g6])
        V(lambda e, l=l: e.tensor_copy(G6[:, l, 4:6, :], ap_bc_mid(GKB[:, l, :], 2)), r=r_gkb + GKB_T.res, w=[r_g6])
    r_nl = pres()
    V(lambda e: e.tensor_tensor(DLP, DLB[:, :, 0::2, :], DLB[:, :, 1::2, :], ALU.mult), r=r_dlb + DLB_T.res, w=DLP_T.res)
    V(lambda e: e.tensor_reduce(out=DLS[:, :, 0:2], in_=DLP, axis=AX.X, op=ALU.add), r=DLP_T.res, w=[r_nl])
    A(lambda e: e.activation(DLS[:, :, 2:4], DLS[:, :, 0:2], AF.Exp), r=[r_nl], w=[r_nl])
    V(lambda e: e.tensor_tensor(NLAM[:], DLS[:, :, 3], DLS[:, :, 2], ALU.subtract), r=[r_nl], w=[r_nl])
    for l in range(2):
        V(lambda e, l=l: e.tensor_scalar(out=NLAM[:, l:l + 1], in0=NLAM[:, l:l + 1], scalar1=-lambda_init(l),
                                         scalar2=None, op0=ALU.add), r=[r_nl], w=[r_nl])
        V(lambda e, l=l: e.tensor_scalar(out=GSUBS[:, l:l + 1], in0=GSUBS[:, l:l + 1], scalar1=1.0 - lambda_init(l),
                                         scalar2=None, op0=ALU.mult), r=[r_pv], w=[r_pv])

    load_win_lru(0)
    wmi = [0]

    def mod(l, wms, psm_b):
        PSM = bank(psm_b)[:, 0:48].rearrange("p (f c) -> p f c", c=2)
        srcs = [w_mod[l].rearrange("(kc p) c -> p kc c", p=128)[:, :, blk * 512:(blk + 1) * 512] for blk in range(6)]

        def dma(blk):
            wm = wms[blk % len(wms)]
            DG(lambda e: e.dma_start(out=wm.bf().rearrange("p (k c) -> p k c", k=8), in_=srcs[blk]), w=wm.res)

        for blk in range(len(wms)):
            dma(blk)
        yield
        for blk in range(6):
            wm = wms[blk % len(wms)]
            wmv = wm.bf().rearrange("p (k c) -> p k c", k=8)
            for fcl in range(4):
                fc = blk * 4 + fcl
                for kc in range(8):
                    T(lambda e, fc=fc, kc=kc, fcl=fcl, wmv=wmv: e.matmul(
                        PSM[:, fc, :], wmv[:, kc, fcl * 128:(fcl + 1) * 128], SC[:, kc, :],
                        start=(kc == 0), stop=(kc == 7)), r=wm.res + [R_SC], w=[RPB[psm_b]], sig=(kc == 7))
            if blk + len(wms) < 6:
                dma(blk + len(wms))
            yield
        V(lambda e: e.tensor_tensor(MODT[:, l], PSM, ap_bc_last(BMT[:, l, :], 2), ALU.add),
          r=[RPB[psm_b]] + PAR, w=[R_MOD[l]])
        V(lambda e: e.scalar_tensor_tensor(out=GS[:, l], in0=MODT[:, l, 8:16, :], scalar=1.0,
                                           in1=ap_bc_last(GPRE[:, l, :], 2), op0=ALU.add, op1=ALU.mult),
          r=[R_MOD[l]] + PAR, w=[R_MOD[l]])
        V(lambda e: e.tensor_tensor(GPT[:, l], MODT[:, l, 16:24, :], ap_bc_last(GPOST[:, l, :], 2), ALU.mult),
          r=[R_MOD[l]] + PAR, w=[R_MOD[l]])
        yield

    for _ in mod(0, WM, 6):
        pass
    load_win_gqa(0)
    for h in range(4):
        load_win_diff(0, h)
    load_wout(0)

    steps = [(0, 0), (0, 1), (1, 0), (1, 1)]

    def rstd_from_sumsq(stat, rs, n_el, cols=1):
        A(lambda e: e.activation(stat[:, 8:8 + cols], stat[:, 0:cols], AF.Ln, scale=1.0 / n_el, bias=EPS), r=[rs], w=[rs])
        A(lambda e: e.activation(stat[:, 16:16 + cols], stat[:, 8:8 + cols], AF.Exp, scale=-0.5), r=[rs], w=[rs])

    def x_load(s, tt):
        DG(lambda e: e.dma_start(out=X[:, tt, :], in_=xin[s][tt * 128:(tt + 1) * 128, :]), w=[RX[tt]])

    XNS = [XN, Tile(XPAD.off, 1024)]
    JUNKS = [JUNK, Tile(RR.off + 512, 512)]

    def prenorm_tile(s, l, tt, XN=XN, JUNK=JUNK):
        c = s
        stat, rs = new_stat()
        A(lambda e: e.activation(JUNK.bf(), X[:, tt, :], AF.Square, accum_out=stat[:, 0:1]), r=[RX[tt]], w=JUNK.res + [rs])
        yield
        rstd_from_sumsq(stat, rs, 1024)
        yield
        V(lambda e: e.tensor_scalar(out=XN.f32(), in0=X[:, tt, :], scalar1=stat[:, 16:17], scalar2=None, op0=ALU.mult),
          r=[RX[tt], rs], w=XN.res)
        yield
        d = 2 + tt % 2
        pst = dbank(d)
        for kc in range(8):
            T(lambda e, kc=kc: e.transpose(pst[:, kc * 128:(kc + 1) * 128], XN.f32()[:, kc * 128:(kc + 1) * 128], IDF[:]),
              r=XN.res + [R_ID], w=[RPB[2 * d + kc // 4]], sig=(kc % 4 == 3))
        yield
        for kc in range(8):
            rr = [RPB[2 * d + kc // 4], R_MOD[l]]
            if kc < 4:
                A(lambda e, kc=kc: e.activation(HT[:, kc, tt * 128:(tt + 1) * 128], pst[:, kc * 128:(kc + 1) * 128], AF.Identity,
                                                scale=GS[:, l, kc, c:c + 1], bias=MODT[:, l, kc, c:c + 1]), r=rr, w=[RHT[tt]])
            else:
                V(lambda e, kc=kc: e.tensor_scalar(out=HT[:, kc, tt * 128:(tt + 1) * 128], in0=pst[:, kc * 128:(kc + 1) * 128],
                                                   scalar1=GS[:, l, kc, c:c + 1], scalar2=MODT[:, l, kc, c:c + 1],
                                                   op0=ALU.mult, op1=ALU.add), r=rr, w=[RHT[tt]])
            if kc % 2 == 1:
                yield

    def prenorm(s, l):
        for tt in range(8):
            for _ in prenorm_tile(s, l, tt):
                pass

    def proj_fm(col0, d, rw):
        ps = dbank(d)
        for half in range(2):
            for kc in range(8):
                T(lambda e, half=half, kc=kc: e.matmul(ps[:, half * 512:(half + 1) * 512], WIN[:, kc, col0:col0 + 128],
                                                       HT[:, kc, half * 512:(half + 1) * 512], start=(kc == 0), stop=(kc == 7)),
                  r=[rw] + RHT[half * 4:half * 4 + 4], w=[RPB[2 * d + half]], sig=(kc == 7))
        return ps

    def all_gates(s, l):
        cols = [(C_LRUG, RW_LRU), (C_LRUG + 128, RW_LRU), (C_GG, RW_GQA), (C_GG + 128, RW_GQA)] + \
               [(C_DG + h * 128, RW_DIFF[h]) for h in range(4)]
        for fc, (col0, rw) in enumerate(cols):
            d = fc % 2
            ps = proj_fm(col0, d, rw)
            yield
            A(lambda e, fc=fc, ps=ps: e.activation(MIXT[:, fc, :], ps, AF.Silu), r=rdb(d), w=RMIX[fc])
            yield

    def par_gen(*gens):
        alive = list(gens)
        while alive:
            for g in list(alive):
                try:
                    next(g)
                except StopIteration:
                    alive.remove(g)
            yield

    def lru(s, l):
        nseq, L = (4, 256) if s == 0 else (1, 1024)
        xpad = XPAD.f32()[:, 0:nseq * (L + 3)].rearrange("p (q t) -> p q t", q=nseq)
        XR = Tile(XPAD.off, 1024)
        bufs = [(RR, II, Y0, 1), (XR, TT, Y1, 0)]

        def v3(t):
            return t.f32().rearrange("p (q t) -> p q t", q=nseq)

        for ch in range(2):
            G(lambda e: e.memset(xpad[:, :, 0:2], 0.0), w=XPAD.res)
            G(lambda e: e.memset(xpad[:, :, L + 2:L + 3], 0.0), w=XPAD.res)
            yield
            ps = proj_fm(C_LRUX + ch * 128, 0, RW_LRU)
            yield
            A(lambda e, ps=ps: e.copy(xpad[:, :, 2:2 + L], ps.rearrange("p (q t) -> p q t", q=nseq)), r=rdb(0), w=XPAD.res)
            yield
            V(lambda e, ch=ch: e.tensor_scalar(out=v3(U), in0=xpad[:, :, 0:L], scalar1=CW[:, l, ch, 0:1],
                                               scalar2=CB[:, l, ch:ch + 1], op0=ALU.mult, op1=ALU.add),
              r=XPAD.res + PAR, w=U.res)
            yield
            for j in range(1, 4):
                V(lambda e, ch=ch, j=j: e.scalar_tensor_tensor(out=v3(U), in0=xpad[:, :, j:j + L], scalar=CW[:, l, ch, j:j + 1],
                                                               in1=v3(U), op0=ALU.mult, op1=ALU.add),
                  r=XPAD.res + U.res + PAR, w=U.res)
                yield
            A(lambda e: e.copy(UBF.bf(), U.f32()), r=U.res, w=UBF.res)
            yield

            def direction(d, ch=ch):
                R, I, Y, pd = bufs[d]
                for typ in range(2):
                    wi = d * 4 + typ * 2 + ch
                    for half in range(2):
                        T(lambda e, wi=wi, half=half: e.matmul(
                            dbank(pd)[:, half * 512:(half + 1) * 512], WBD[:, l, wi, :], UBF.bf()[:, half * 512:(half + 1) * 512],
                            start=True, stop=True), r=UBF.res + PAR, w=[RPB[2 * pd + half]])
                    yield
                    dst = R if typ == 0 else I
                    bias = (BA if typ == 0 else BXX)[:, l, d, ch:ch + 1]
                    A(lambda e, dst=dst, bias=bias: e.activation(dst.f32(), dbank(pd), AF.Sigmoid, bias=bias),
                      r=rdb(pd) + PAR, w=dst.res)
                    yield
                A(lambda e: e.activation(R.f32(), R.f32(), AF.Exp, scale=CEXP[:, l, d, ch:ch + 1]), r=R.res + PAR, w=R.res)
                yield
                V(lambda e: e.tensor_tensor(Y.f32(), R.f32(), R.f32(), ALU.mult), r=R.res, w=Y.res)
                yield
                A(lambda e: e.activation(Y.f32(), Y.f32(), AF.Sqrt, scale=-1.0, bias=1.0), r=Y.res, w=Y.res)
                yield
                if s == 0:
                    pos = 0 if d == 0 else L - 1
                    V(lambda e, pos=pos: e.memset(v3(R)[:, :, pos:pos + 1], 0.0), r=R.res, w=R.res)
                V(lambda e: e.tensor_tensor(I.f32(), I.f32(), U.f32(), ALU.mult), r=I.res + U.res, w=I.res)
                yield
                V(lambda e: e.tensor_tensor(I.f32(), I.f32(), Y.f32(), ALU.mult), r=I.res + Y.res, w=I.res)
                yield
                init = 0.0 if s == 0 else H0[:, l, d, ch:ch + 1]
                if d == 0:
                    V(lambda e: e.tensor_tensor_scan(Y.f32(), R.f32(), I.f32(), init, ALU.mult, ALU.add),
                      r=R.res + I.res + PAR, w=Y.res)
                else:
                    V(lambda e: e.tensor_tensor_scan(Y.f32()[:, ::-1], R.f32()[:, ::-1], I.f32()[:, ::-1],
                                                     init, ALU.mult, ALU.add), r=R.res + I.res + PAR, w=Y.res)
                yield
                if s == 0:
                    pos = L - 1 if d == 0 else 0
                    srcv = v3(Y)[:, :, pos]
                    dstv = nlru[:, l, d, ch * 128:(ch + 1) * 128].rearrange("q p -> p q")
                    D(lambda e: e.dma_start(out=dstv, in_=srcv, allow_slow_non_contiguous=True), r=Y.res, out=True)

            yield from par_gen(direction(0), direction(1))
            V(lambda e: e.tensor_tensor(Y0.f32(), Y0.f32(), Y1.f32(), ALU.add), r=Y0.res + Y1.res, w=Y0.res)
            yield
            V(lambda e, ch=ch: e.tensor_tensor(MIXT[:, ch, :], Y0.f32(), MIXT[:, ch, :], ALU.mult), r=Y0.res + RMIX[ch], w=RMIX[ch])
            yield

    pt_i = [0]

    def rope_views(src_ap, nh, tt):
        x5 = src_ap.rearrange("p (h b f d) -> p h b f d", h=nh, b=2, f=2)
        xs5 = x5[:, :, :, ::-1, :]
        c5 = ap_bc_mid(ROPEC[:, tt, :].rearrange("p (b f d) -> p b f d", b=2, f=2), nh)
        s5 = ap_bc_mid(ROPES[:, tt, :].rearrange("p (b f d) -> p b f d", b=2, f=2), nh)
        return x5, xs5, c5, s5

    PJ_B, TR_B = 6, 7

    def interleave(gens, weights=None):
        alive = [[g, (weights[i] if weights else 1)] for i, g in enumerate(gens) if g is not None]
        while alive:
            for item in list(alive):
                g, w = item
                for _ in range(w):
                    try:
                        next(g)
                    except StopIteration:
                        alive.remove(item)
                        break

    def attn_core(s, qb, qt_ap, kt_ap, va_of, W, rd_extra):
        chunks = [2 * qb, 2 * qb + 1] if s == 0 else list(range(12))
        n = len(chunks)
        W1 = W + 1
        pts = {}

        def scores(ci):
            kc = chunks[ci]
            b0 = 2 if s == 0 else 2 + 2 * (ci % 2)
            for m in range(2):
                T(lambda e, m=m, kc=kc, b0=b0: e.matmul(bank(b0 + m)[:, 0:256], kt_ap[m * 64:(m + 1) * 64, kc, :],
                                                        qt_ap[m * 64:(m + 1) * 64, qb * 256:(qb + 1) * 256], start=True, stop=True),
                  r=rd_extra, w=[RPB[b0 + m]], sig=(m == 1))
            pt = PTB[pt_i[0] % len(PTB)]; pt_i[0] += 1
            pts[ci] = pt
            sc2 = PS[:, b0 * 512:(b0 + 2) * 512].rearrange("p (m c) -> p m c", m=2)[:, :, 0:256]
            A(lambda e, sc2=sc2, pt=pt: e.activation(pt.bf().rearrange("p (m c) -> p m c", m=2), sc2, AF.Exp, scale=0.125),
              r=[RPB[b0], RPB[b0 + 1]], w=pt.res)

        def pv(ci):
            kc = chunks[ci]
            pt = pts[ci]
            for m in range(2):
                for qt in range(2):
                    last = (m == 1 and qt == 1)
                    T(lambda e, m=m, qt=qt, kc=kc, pt=pt, ci=ci: e.matmul(
                        bank(m)[:, qt * W1:(qt + 1) * W1], pt.bf()[:, m * 256 + qt * 128:m * 256 + (qt + 1) * 128],
                        va_of(m)[:, kc, 0:W1], start=(ci == 0 and qt == 0), stop=(ci == n - 1),
                        skip_group_check=True), r=pt.res + rd_extra, w=[RPB[m]], sig=last)

        scores(0)
        yield
        for ci in range(n):
            if ci + 1 < n:
                scores(ci + 1)
            pv(ci)
            yield

    class BufSet:
        def __init__(self, base):
            self.QK = Tile(base, 1792)
            self.VA = Tile(base + 1792, 1024)
            self.GATE = Tile(base + 2816, 1024)

    BS_A = BufSet(QT.off)
    BS_B = BufSet(II.off)
    _gb = II.off + 2816
    QKBD = QKB + [Tile(_gb, 192), Tile(_gb + 192, 192)]
    OSTD = OST + [Tile(_gb + 384, 256), Tile(_gb + 640, 256)]

    def proj_gate(col0, rw, dst_of_half, gres):
        for half in range(2):
            for kc in range(8):
                T(lambda e, half=half, kc=kc: e.matmul(bank(PJ_B), WIN[:, kc, col0:col0 + 128],
                                                       HT[:, kc, half * 512:(half + 1) * 512], start=(kc == 0), stop=(kc == 7)),
                  r=[rw] + RHT[half * 4:half * 4 + 4], w=[RPB[PJ_B]], sig=(kc == 7))
            yield
            A(lambda e, half=half: e.activation(dst_of_half(half), bank(PJ_B), AF.Silu), r=[RPB[PJ_B]], w=gres)
            yield

    def cache_k_load(src_ap):
        DG(lambda e: e.dma_start(out=CKB.bf().rearrange("p (c d) -> p c d", c=4), in_=src_ap), w=CKB.res)
        ptr = bankbf(PJ_B)
        for c4 in range(4):
            T(lambda e, c4=c4: e.transpose(ptr[:, c4 * 128:(c4 + 1) * 128], CKB.bf()[:, c4 * 128:(c4 + 1) * 128], IDB[:]),
              r=CKB.res + [R_ID], w=[RPB[PJ_B]], sig=(c4 == 3))
        return ptr

    def gqa_unit(s, l, bs):
        koff = 0 if s == 0 else 4
        qkall = bs.QK.bf()
        qt2 = qkall[:, 0:2048].rearrange("p (u t) -> p u t", u=2)
        kt = qkall[:, 2048:2048 + 1536].rearrange("p (k t) -> p k t", t=128)
        va = bs.VA.bf()[:, 0:2 * 12 * 66].rearrange("p (v k w) -> p v k w", v=2, k=12)
        allres = bs.QK.res + bs.VA.res

        def prep_head():
            G(lambda e: e.memset(va[:, :, :, 64:65], 1.0), w=bs.VA.res)
            yield
            if s == 1:
                for kv in range(2):
                    DG(lambda e, kv=kv: e.dma_start(out=va[:, kv, 0:4, 0:64],
                                                    in_=cgv[l].rearrange("(c p) (v d) -> p v c d", p=128, v=2)[:, kv]), w=bs.VA.res)
                ptr = cache_k_load(cgk[l].rearrange("(c p) d -> p c d", p=128))
                yield
                A(lambda e: e.copy(qkall[:, 2048:2048 + 512], ptr[:, 0:512]), r=[RPB[PJ_B]], w=bs.QK.res)
                yield

        def prep_tile(tt):
            par = tt % 2
            PJ = 6 if par == 0 else 4
            TRB = PJ + 1
            pj = bank(PJ)
            QK, T1, T2 = GQT[par]
            for kc in range(8):
                T(lambda e, kc=kc: e.matmul(pj[:, 0:512], HT[:, kc, tt * 128:(tt + 1) * 128], WIN[:, kc, 512:1024],
                                            start=(kc == 0), stop=(kc == 7)), r=[RHT[tt], RW_GQA], w=[RPB[PJ]], sig=(kc == 7))
            yield
            stat, rs = new_stat()
            A(lambda e: e.activation(T1.f32(), pj[:, 0:384], AF.Square), r=[RPB[PJ]], w=T1.res)
            yield
            V(lambda e: e.tensor_reduce(out=stat[:, 0:6], in_=T1.f32().rearrange("p (h d) -> p h d", h=6),
                                        axis=AX.X, op=ALU.add), r=T1.res, w=[rs])
            yield
            rstd_from_sumsq(stat, rs, 64, cols=6)
            yield
            qk3 = QK.f32().rearrange("p (h d) -> p h d", h=6)
            V(lambda e: e.tensor_tensor(qk3, pj[:, 0:384].rearrange("p (h d) -> p h d", h=6),
                                        ap_bc_last(stat[:, 16:22], 64), ALU.mult), r=[RPB[PJ], rs], w=QK.res)
            yield
            A(lambda e: e.copy(va[:, :, koff + tt, 0:64], pj[:, 384:512].rearrange("p (v d) -> p v d", v=2)),
              r=[RPB[PJ]], w=bs.VA.res)
            ost = OST[tt % 2]
            if s == 0:
                A(lambda e: e.copy(ost.f32()[:, 0:128], pj[:, 384:512]), r=[RPB[PJ]], w=ost.res)
            yield
            V(lambda e: e.tensor_tensor(qk3, qk3, G6[:, l], ALU.mult), r=QK.res + PAR, w=QK.res)
            yield
            qkb = QKB[par]
            qdst = qkb.bf()[:, 0:256].rearrange("p (b a d) -> p a b d", a=2, b=2)
            kdst = qkb.bf()[:, 256:384]
            if s == 0:
                seq, tq = tt // 2, tt % 2
                D(lambda e: e.dma_start(out=ngk[seq, l, tq * 128:(tq + 1) * 128, :], in_=QK.f32()[:, 256:384]),
                  r=QK.res, out=True)
                D(lambda e: e.dma_start(out=ngv[seq, l, tq * 128:(tq + 1) * 128, :], in_=ost.f32()[:, 0:128]),
                  r=ost.res, out=True)
                V(lambda e: e.tensor_copy(qdst, QK.f32()[:, 0:256].rearrange("p (a b d) -> p a b d", a=2, b=2)),
                  r=QK.res, w=qkb.res)
                V(lambda e: e.tensor_copy(kdst, QK.f32()[:, 256:384]), r=QK.res, w=qkb.res)
                yield
            else:
                x5, xs5, c5, s5 = rope_views(QK.f32(), 6, tt)
                t15 = T1.f32().rearrange("p (h b f d) -> p h b f d", h=6, b=2, f=2)
                t25 = T2.f32().rearrange("p (h b f d) -> p h b f d", h=6, b=2, f=2)
                V(lambda e: e.tensor_tensor(t15, x5, c5, ALU.mult), r=QK.res + [R_ROPE], w=T1.res)
                yield
                V(lambda e: e.tensor_tensor(t25, xs5, s5, ALU.mult), r=QK.res + [R_ROPE], w=T2.res)
                yield
                G(lambda e: e.tensor_tensor(qdst, T1.f32()[:, 0:256].rearrange("p (a b d) -> p a b d", a=2, b=2),
                                            T2.f32()[:, 0:256].rearrange("p (a b d) -> p a b d", a=2, b=2), ALU.add),
                  r=T1.res + T2.res, w=qkb.res)
                G(lambda e: e.tensor_tensor(kdst, T1.f32()[:, 256:384], T2.f32()[:, 256:384], ALU.add),
                  r=T1.res + T2.res, w=qkb.res)
                yield
            ptr = bankbf(TRB)
            for j in range(3):
                T(lambda e, j=j: e.transpose(ptr[:, j * 128:(j + 1) * 128], qkb.bf()[:, j * 128:(j + 1) * 128], IDB[:]),
                  r=qkb.res + [R_ID], w=[RPB[TRB]], sig=(j == 2))
            yield
            if s == 0:
                dst3 = AP(qkall.tensor, qkall[:, tt * 128:tt * 128 + 1].offset, [list(qkall.ap[0]), [1024, 3], [1, 128]])
                A(lambda e: e.copy(dst3, ptr[:, 0:384].rearrange("p (u t) -> p u t", u=3)), r=[RPB[TRB]], w=bs.QK.res)
            else:
                A(lambda e: e.copy(qt2[:, :, tt * 128:(tt + 1) * 128], ptr[:, 0:256].rearrange("p (u t) -> p u t", u=2)),
                  r=[RPB[TRB]], w=bs.QK.res)
                A(lambda e: e.copy(kt[:, koff + tt, :], ptr[:, 256:384]), r=[RPB[TRB]], w=bs.QK.res)
            yield

        def phases(qb):
            lst = []
            for u in range(2):
                def head(u=u):
                    stat, rs = new_stat()
                    for m in range(2):
                        accv = bank(m)[:, 0:130].rearrange("p (q w) -> p q w", q=2)
                        V(lambda e, m=m, accv=accv: e.reciprocal(stat[:, 2 * m:2 * m + 2], accv[:, :, 64]), r=[RPB[m]], w=[rs])
                    for m in range(2):
                        hd = u + 2 * m
                        accv = bank(m)[:, 0:130].rearrange("p (q w) -> p q w", q=2)
                        obv = OBT.bf().rearrange("p (q h d) -> p q h d", q=2, h=4)[:, :, hd, :]
                        V(lambda e, m=m, accv=accv, obv=obv: e.tensor_tensor(
                            obv, accv[:, :, 0:64], ap_bc_last(stat[:, 2 * m:2 * m + 2], 64), ALU.mult),
                          r=[RPB[m], rs], w=OBT.res)
                lst.append((attn_core(s, qb, qt2[:, u, :], kt, lambda m: va[:, m], 64, allres), head))
            return lst

        def tail(qb):
            pso = bankbf(2)[:, 512:1024]
            for qt in range(2):
                for pair in range(2):
                    T(lambda e, qt=qt, pair=pair: e.transpose(pso[:, (pair * 2 + qt) * 128:(pair * 2 + qt + 1) * 128],
                                                              OBT.bf()[:, qt * 256 + pair * 128:qt * 256 + (pair + 1) * 128], IDB[:]),
                      r=OBT.res + [R_ID], w=[RPB[2]], sig=(qt == 1 and pair == 1))
            yield
            for pair in range(2):
                V(lambda e, pair=pair: e.tensor_tensor(MIXT[:, 2 + pair, qb * 256:(qb + 1) * 256], pso[:, pair * 256:(pair + 1) * 256],
                                                       MIXT[:, 2 + pair, qb * 256:(qb + 1) * 256], ALU.mult),
                  r=[RPB[2], RMIX[2 + pair][2 * qb], RMIX[2 + pair][2 * qb + 1]],
                  w=[RMIX[2 + pair][2 * qb], RMIX[2 + pair][2 * qb + 1]])
            yield

        return prep_head, prep_tile, phases, tail

    def diff_unit(s, l, h, bs):
        NPAR = 4 if s == 0 else 2
        koff = 0 if s == 0 else 4
        qkall = bs.QK.bf()
        qt = qkall[:, 0:1024]
        kt = qkall[:, 1024:1024 + 1536].rearrange("p (k t) -> p k t", t=128)
        va = bs.VA.bf()[:, 0:12 * 130].rearrange("p (k w) -> p k w", k=12)
        allres = bs.QK.res + bs.VA.res
        rhs_all = WIN[:, :, C_DQ:C_DQ + 1536].rearrange("p k (s c) -> p k s c", s=3)[:, :, :, h * 128:(h + 1) * 128]

        def prep_head():
            G(lambda e: e.memset(va[:, :, 128:129], 1.0), w=bs.VA.res)
            yield
            if s == 1:
                DG(lambda e: e.dma_start(out=va[:, 0:4, 0:128],
                                         in_=cdv[l].rearrange("(c p) d -> p c d", p=128)[:, :, h * 128:(h + 1) * 128]), w=bs.VA.res)
                ptr = cache_k_load(cdk[l].rearrange("(c p) d -> p c d", p=128)[:, :, h * 128:(h + 1) * 128])
                yield
                A(lambda e: e.copy(qkall[:, 1024:1024 + 512], ptr[:, 0:512]), r=[RPB[PJ_B]], w=bs.QK.res)
                yield

        def prep_tile(tt):
            par = tt % NPAR
            PJ = (6, 7, 4, 5)[par]
            pj = bank(PJ)
            t1 = T1S[par % 2]; t2 = T2S[par % 2]
            for kc in range(8):
                T(lambda e, kc=kc: e.matmul(pj[:, 0:384], HT[:, kc, tt * 128:(tt + 1) * 128], rhs_all[:, kc],
                                            start=(kc == 0), stop=(kc == 7)), r=[RHT[tt], RW_DIFF[h]], w=[RPB[PJ]], sig=(kc == 7))
            yield
            qkb = QKBD[par]
            if s == 0:
                seq, tq = tt // 2, tt % 2
                ost = OSTD[par]
                A(lambda e: e.copy(ost.f32(), pj[:, 128:384]), r=[RPB[PJ]], w=ost.res)
                yield
                V(lambda e: e.tensor_copy(qkb.bf()[:, 0:256], pj[:, 0:256]), r=[RPB[PJ]], w=qkb.res)
                yield
                D(lambda e: e.dma_start(out=ndk[seq, l, tq * 128:(tq + 1) * 128, h * 128:(h + 1) * 128],
                                        in_=ost.f32()[:, 0:128]), r=ost.res, out=True)
                D(lambda e: e.dma_start(out=ndv[seq, l, tq * 128:(tq + 1) * 128, h * 128:(h + 1) * 128],
                                        in_=ost.f32()[:, 128:256]), r=ost.res, out=True)
            else:
                x5, xs5, c5, s5 = rope_views(pj[:, 0:256], 4, tt)
                t15 = t1.f32().rearrange("p (h b f d) -> p h b f d", h=4, b=2, f=2)
                t25 = t2.f32().rearrange("p (h b f d) -> p h b f d", h=4, b=2, f=2)
                V(lambda e: e.tensor_tensor(t15, x5, c5, ALU.mult), r=[RPB[PJ], R_ROPE], w=t1.res)
                yield
                V(lambda e: e.tensor_tensor(t25, xs5, s5, ALU.mult), r=[RPB[PJ], R_ROPE], w=t2.res)
                yield
                V(lambda e: e.tensor_tensor(qkb.bf()[:, 0:256], t1.f32(), t2.f32(), ALU.add), r=t1.res + t2.res, w=qkb.res)
                yield
            A(lambda e: e.copy(va[:, koff + tt, 0:128], pj[:, 256:384]), r=[RPB[PJ]], w=bs.VA.res)
            yield
            ptr = bankbf(PJ)[:, 768:1024]
            for j in range(2):
                T(lambda e, j=j: e.transpose(ptr[:, j * 128:(j + 1) * 128], qkb.bf()[:, j * 128:(j + 1) * 128], IDB[:]),
                  r=qkb.res + [R_ID], w=[RPB[PJ]], sig=(j == 1))
            yield
            dst2 = AP(qkall.tensor, qkall[:, tt * 128:tt * 128 + 1].offset, [list(qkall.ap[0]), [1024 + koff * 128, 2], [1, 128]])
            A(lambda e: e.copy(dst2, ptr[:, 0:256].rearrange("p (u t) -> p u t", u=2)), r=[RPB[PJ]], w=bs.QK.res)
            yield

        fin = {}

        def phases(qb):
            def head():
                acc0 = bank(0)[:, 0:258].rearrange("p (q w) -> p q w", q=2)
                acc1 = bank(1)[:, 0:258].rearrange("p (q w) -> p q w", q=2)
                stat, rs = new_stat()
                V(lambda e: e.reciprocal(stat[:, 0:2], acc0[:, :, 128]), r=[RPB[0]], w=[rs])
                V(lambda e: e.reciprocal(stat[:, 2:4], acc1[:, :, 128]), r=[RPB[1]], w=[rs])
                V(lambda e: e.tensor_scalar(out=stat[:, 4:6], in0=stat[:, 2:4], scalar1=NLAM[:, l:l + 1], scalar2=None, op0=ALU.mult),
                  r=[rs] + PAR, w=[rs])
                o0 = O0T.f32().rearrange("p (q d) -> p q d", q=2); do = DOT.f32().rearrange("p (q d) -> p q d", q=2)
                V(lambda e: e.tensor_tensor(o0, acc0[:, :, 0:128], ap_bc_last(stat[:, 0:2], 128), ALU.mult), r=[RPB[0], rs], w=O0T.res)
                V(lambda e: e.tensor_tensor(do, acc1[:, :, 0:128], ap_bc_last(stat[:, 4:6], 128), ALU.mult), r=[RPB[1], rs], w=DOT.res)
            return [(attn_core(s, qb, qt, kt, lambda m: va, 128, allres), head)]

        def tail(qb):
            o0 = O0T.f32().rearrange("p (q d) -> p q d", q=2); do = DOT.f32().rearrange("p (q d) -> p q d", q=2)
            V(lambda e: e.tensor_tensor(do, do, o0, ALU.add), r=DOT.res + O0T.res, w=DOT.res)
            yield
            stat2, rs2 = new_stat()
            for qi in range(2):
                A(lambda e, qi=qi: e.activation(O0T.f32()[:, qi * 128:(qi + 1) * 128], DOT.f32()[:, qi * 128:(qi + 1) * 128],
                                                AF.Square, accum_out=stat2[:, qi:qi + 1]), r=DOT.res, w=O0T.res + [rs2])
            yield
            rstd_from_sumsq(stat2, rs2, 128, cols=2)
            yield
            dob = DOBT.bf().rearrange("p (q d) -> p q d", q=2)
            V(lambda e: e.tensor_tensor(dob, do, ap_bc_last(stat2[:, 16:18], 128), ALU.mult), r=DOT.res + [rs2], w=DOBT.res)
            yield
            pso = bankbf(2)[:, 512:768]
            for qi in range(2):
                T(lambda e, qi=qi: e.transpose(pso[:, qi * 128:(qi + 1) * 128], DOBT.bf()[:, qi * 128:(qi + 1) * 128], IDB[:]),
                  r=DOBT.res + [R_ID], w=[RPB[2]], sig=(qi == 1))
            yield
            V(lambda e: e.scalar_tensor_tensor(out=MIXT[:, 4 + h, qb * 256:(qb + 1) * 256], in0=pso[:, 0:256], scalar=GSUBS[:, l:l + 1],
                                               in1=MIXT[:, 4 + h, qb * 256:(qb + 1) * 256], op0=ALU.mult, op1=ALU.mult),
              r=[RPB[2], RMIX[4 + h][2 * qb], RMIX[4 + h][2 * qb + 1]] + PAR, w=[RMIX[4 + h][2 * qb], RMIX[4 + h][2 * qb + 1]])
            yield

        return prep_head, prep_tile, phases, tail

    def attention(s, l, nxt, lru_gen):
        sets = [BS_A, BS_B]
        units = [gqa_unit(s, l, sets[0])] + [diff_unit(s, l, h, sets[(h + 1) % 2]) for h in range(4)]
        loads = [lambda: load_win_gqa(nxt)] + [lambda h=h: load_win_diff(nxt, h) for h in range(4)]

        def prep_gen(u, tiles=range(8), head=True):
            if head:
                yield from u[0]()
            for tt in tiles:
                yield from u[1](tt)

        def core_gen(u):
            pend = [None]
            kstep = 3 if s == 0 else 1

            def step_pending(n):
                for _ in range(n):
                    if pend[0] is None:
                        return
                    try:
                        next(pend[0])
                    except StopIteration:
                        pend[0] = None

            for qb in range(4):
                for attn_gen, head in u[2](qb):
                    for _ in attn_gen:
                        step_pending(kstep)
                        yield
                    step_pending(1000)
                    head()
                    yield
                pend[0] = u[3](qb)
            while pend[0] is not None:
                step_pending(1)
                yield

        for _ in units[0][0]():
            pass
        interleave([prep_gen(units[0], range(0, 8, 2), False), prep_gen(units[0], range(1, 8, 2), False), lru_gen])
        if nxt is not None:
            load_win_lru(nxt)
            loads[0]()
        for i, u in enumerate(units):
            nx = units[i + 1] if i + 1 < len(units) else None
            if nx is None:
                interleave([core_gen(u)])
            else:
                npar = 4 if s == 0 else 2
                interleave([core_gen(u)] + [prep_gen(nx, range(p_, 8, npar), p_ == 0) for p_ in range(npar)])
            if nx is not None and nxt is not None:
                loads[i + 1]()

    def out_proj(s, l, nstep):
        c = s
        psg = dbank(2)
        for kc in range(8):
            bc = BCT[kc % 2]
            V(lambda e, kc=kc, bc=bc: e.tensor_copy(bc.f32(), ap_bc_last(GPT[:, l, kc, c], 128)), r=[R_MOD[l]], w=bc.res)
            T(lambda e, kc=kc, bc=bc: e.matmul(psg[:, kc * 128:(kc + 1) * 128], bc.f32(), IDF[:], start=True, stop=True),
              r=bc.res + [R_ID], w=[RPB[4 + kc // 4]])
        A(lambda e: e.copy(GPB.f32(), psg), r=rdb(2), w=GPB.res)

        def mm(tt):
            d = tt % 2
            psy = dbank(d)
            for half in range(2):
                for fc in range(8):
                    T(lambda e, half=half, fc=fc: e.matmul(
                        psy[:, half * 512:(half + 1) * 512], MIXT[:, fc, tt * 128:(tt + 1) * 128], WOUT[:, fc, half * 512:(half + 1) * 512],
                        start=(fc == 0), stop=(fc == 7)), r=[RMIX[fc][tt], RW_OUT], w=[RPB[2 * d + half]], sig=(fc == 7))

        def post(tt):
            d = tt % 2
            psy = dbank(d)
            stat, rs = new_stat()
            A(lambda e: e.activation(JUNK2.bf(), psy, AF.Square, accum_out=stat[:, 0:1]), r=rdb(d), w=JUNK2.res + [rs])
            yield
            V(lambda e: e.tensor_tensor(TMP.f32(), psy, GPB.f32(), ALU.mult), r=rdb(d) + GPB.res, w=TMP.res)
            yield
            if tt + 2 < 8:
                mm(tt + 2)
            rstd_from_sumsq(stat, rs, 1024)
            yield
            V(lambda e: e.scalar_tensor_tensor(out=X[:, tt, :], in0=TMP.f32(), scalar=stat[:, 16:17], in1=X[:, tt, :],
                                               op0=ALU.mult, op1=ALU.add), r=TMP.res + [rs, RX[tt]], w=[RX[tt]])
            yield
            if l == 1:
                D(lambda e: e.dma_start(out=yout[s][tt * 128:(tt + 1) * 128, :], in_=X[:, tt, :]), r=[RX[tt]], out=True)
            yield

        posted = [-1]

        new_group = nstep is not None and nstep[1] == 0

        def post_chain():
            for tt in range(8):
                yield from post(tt)
                posted[0] = tt
                if new_group:
                    x_load(nstep[0], tt)

        def pn_chain(par):
            if nstep is None:
                return
            for tt in range(par, 8, 2):
                while posted[0] < tt:
                    yield
                yield from prenorm_tile(nstep[0], nstep[1], tt, XNS[par], JUNKS[par])

        mm(0)
        mm(1)
        if new_group:
            for _ in post_chain():
                pass
            interleave([pn_chain(0), pn_chain(1)])
        else:
            interleave([post_chain(), pn_chain(0), pn_chain(1)])

    prenorm(*steps[0])
    for si, (s, l) in enumerate(steps):
        nstep = steps[si + 1] if si + 1 < len(steps) else None
        nxt = nstep[1] if nstep is not None else None
        def gates_then_lru(s=s, l=l, si=si):
            g = mod(1, [Tile(U.off, 2048), Tile(II.off, 2048)], 0) if si == 0 else None
            if g is not None:
                next(g)
            yield from all_gates(s, l)
            if g is not None:
                yield from g
            yield from lru(s, l)

        attention(s, l, nxt, gates_then_lru())
        out_proj(s, l, nstep)
        if nxt is not None:
            load_wout(nxt)

    P.emit(nc, st)
    st.close()
    return nc


def _rope_tables():
    inv = np.power(np.float32(10000.0), -np.arange(16, dtype=np.float32) / np.float32(16)).astype(np.float32)
    t = np.arange(1024)
    row = (t // 64).astype(np.float32)
    col = (t % 64).astype(np.float32)
    ar = (row[:, None] * inv[None]).astype(np.float32)
    ac = (col[:, None] * inv[None]).astype(np.float32)
    C = np.concatenate([np.cos(ar), np.cos(ar), np.cos(ac), np.cos(ac)], axis=1).astype(np.float32)
    S = np.concatenate([-np.sin(ar), np.sin(ar), -np.sin(ac), np.sin(ac)], axis=1).astype(np.float32)
    return np.ascontiguousarray(C), np.ascontiguousarray(S)


_NC_CACHE = {}


def kernel(x_prompt, x_sample, cache_gqa_k, cache_gqa_v, cache_diff_k, cache_diff_v, state_lru,
           c, c_ctx, w_mod, b_mod, g_pre, g_post, w_in, w_out, lru_conv_w, lru_conv_b,
           lru_wa, lru_ba, lru_wx, lru_bx, lru_lambda, gqa_gq, gqa_gk, diff_lam, diff_gsub):
    f = lambda a: np.ascontiguousarray(np.asarray(a, dtype=np.float32))
    x_prompt, x_sample = f(x_prompt), f(x_sample)
    ropeC, ropeS = _rope_tables()
    shared = {
        "w_mod": f(w_mod), "b_mod": f(b_mod), "g_pre": f(g_pre), "g_post": f(g_post),
        "w_in": f(w_in), "w_out": f(w_out), "conv_w": f(lru_conv_w), "conv_b": f(lru_conv_b),
        "lru_wa": f(lru_wa), "lru_ba": f(lru_ba), "lru_wx": f(lru_wx), "lru_bx": f(lru_bx),
        "lru_lambda": f(lru_lambda), "gqa_gq": f(gqa_gq), "gqa_gk": f(gqa_gk),
        "diff_lam": f(diff_lam), "diff_gsub": f(diff_gsub),
        "ident": np.eye(128, dtype=np.float32), "ropec": ropeC, "ropes": ropeS,
    }
    c = f(c); c_ctx = f(c_ctx)
    fm = lambda a: np.moveaxis(a.reshape(a.shape[:-1] + (a.shape[-1] // 128, 128)), -1, 0)
    b_mod_, g_pre_, g_post_ = shared["b_mod"], shared["g_pre"], shared["g_post"]
    cw_ = fm(shared["conv_w"])
    common = [fm(b_mod_).reshape(128, -1), fm(g_pre_).reshape(128, -1), fm(g_post_).reshape(128, -1),
              np.transpose(cw_, (0, 1, 3, 2)).reshape(128, -1), fm(shared["conv_b"]).reshape(128, -1),
              fm(shared["lru_ba"]).reshape(128, -1), fm(shared["lru_bx"]).reshape(128, -1),
              fm(shared["lru_lambda"]).reshape(128, -1)]
    gsub_ = np.ascontiguousarray(shared["diff_gsub"].T)
    wbd = np.zeros((128, 2, 8, 128), np.float32)
    for typ, wsrc in enumerate((shared["lru_wa"], shared["lru_wx"])):
        for l_ in range(2):
            for d_ in range(2):
                for n_ in range(4):
                    ch_, b_ = n_ // 2, n_ % 2
                    wbd[b_ * 64:(b_ + 1) * 64, l_, d_ * 4 + typ * 2 + ch_, b_ * 64:(b_ + 1) * 64] = wsrc[l_, d_, n_]
    shared["wbd"] = wbd.reshape(128, -1)
    for k_ in ("b_mod", "g_pre", "g_post", "conv_w", "conv_b", "lru_wa", "lru_ba", "lru_wx", "lru_bx", "lru_lambda", "diff_gsub"):
        del shared[k_]
    in_maps = []
    for i in range(N_CORES):
        m = dict(shared)
        cond_i = np.stack([c_ctx, c[i]], axis=0)
        condT = np.transpose(fm(cond_i), (0, 2, 1)).reshape(128, -1)
        h0_ = fm(f(state_lru[i])).reshape(128, -1)
        m["pvec"] = np.ascontiguousarray(np.concatenate([condT] + common + [h0_, gsub_], axis=1), dtype=np.float32)
        assert m["pvec"].shape == (128, NPV)
        m["xp"] = x_prompt[4 * i:4 * i + 4].reshape(1024, 1024)
        m["xs"] = x_sample[i]
        m["cgk"] = f(cache_gqa_k[i]).reshape(2, 512, 128)
        m["cgv"] = f(cache_gqa_v[i]).reshape(2, 512, 128)
        m["cdk"] = f(cache_diff_k[i]).reshape(2, 512, 512)
        m["cdv"] = f(cache_diff_v[i]).reshape(2, 512, 512)
        in_maps.append(m)
    if "nc" not in _NC_CACHE:
        _NC_CACHE["nc"] = build()
    nc = _NC_CACHE["nc"]
    ncore = int(os.environ.get("KCORES", str(N_CORES)))
    res = run_bass_kernel_spmd(nc, in_maps[:ncore], core_ids=list(range(ncore)))
    R = list(res.results) + [res.results[0]] * (N_CORES - ncore)
    y_prompt = np.concatenate([R[i]["yp"].reshape(4, 256, 1024) for i in range(N_CORES)], axis=0)
    y_sample = np.stack([R[i]["ys"] for i in range(N_CORES)], axis=0)
    new_gqa_k = np.concatenate([R[i]["ngk"].reshape(4, 2, 256, 2, 64) for i in range(N_CORES)], axis=0)
    new_gqa_v = np.concatenate([R[i]["ngv"].reshape(4, 2, 256, 2, 64) for i in range(N_CORES)], axis=0)
    new_diff_k = np.concatenate([R[i]["ndk"].reshape(4, 2, 256, 4, 2, 64) for i in range(N_CORES)], axis=0)
    new_diff_v = np.concatenate([R[i]["ndv"].reshape(4, 2, 256, 4, 128) for i in range(N_CORES)], axis=0)
    new_lru = np.concatenate([R[i]["nlru"] for i in range(N_CORES)], axis=0)
    return (y_prompt.astype(np.float32), y_sample.astype(np.float32), new_gqa_k.astype(np.float32),
            new_gqa_v.astype(np.float32), new_diff_k.astype(np.float32), new_diff_v.astype(np.float32),
            new_lru.astype(np.float32))
```

```python
import math
import os
from contextlib import ExitStack

import numpy as np
import concourse.bass as bass
import concourse.mybir as mybir
from concourse.alu_op_type import AluOpType as ALU
from concourse.ap import AP
from concourse.bass_utils import run_bass_kernel_spmd

F32 = mybir.dt.float32
BF16 = mybir.dt.bfloat16
AF = mybir.ActivationFunctionType
AX = mybir.AxisListType

EPS = 1e-6
NPV = 150
N_CORES = 8
ENGS = ["sync", "scalar", "gpsimd", "vector", "tensor"]
N_DMA_SEMS = 64
N_HW_SEMS = 40
SAME_ENGINE_SYNC = True


class Res:
    __slots__ = ("w", "r", "x")

    def __init__(self, x=False):
        self.w = None
        self.r = {}
        self.x = x


class Prog:
    def __init__(self):
        self.q = {e: [] for e in ENGS}
        self.cnt = {e: 0 for e in ENGS}
        self.dma_val = [0] * N_DMA_SEMS
        self.dma_next = {}
        self.waited = {e: {} for e in ENGS}
        self.out_tokens = []
        self.last_sig = {e: True for e in ENGS}

    def op(self, eng, fn, reads=(), writes=(), dma=False, is_output=False, signal=True):
        deps = {}
        same = SAME_ENGINE_SYNC and eng != "tensor"

        def add(tok, same_ok):
            if tok is None:
                return
            k, v = tok
            if k == ("c", eng) and not same_ok:
                return
            if deps.get(k, 0) < v:
                deps[k] = v

        for r in reads:
            add(r.w, same)
            if r.x:
                for k, v in r.r.items():
                    add((k, v), False)
        for w in writes:
            add(w.w, same)
            for k, v in w.r.items():
                add((k, v), same)
        if dma:
            lo, hi = (0, N_HW_SEMS) if eng == "sync" else (N_HW_SEMS, N_DMA_SEMS)
            si = self.dma_next.get(eng, lo)
            self.dma_next[eng] = lo + (si + 1 - lo) % (hi - lo)
            if self.dma_val[si] > 0:
                k = ("d", si)
                if deps.get(k, 0) < self.dma_val[si]:
                    deps[k] = self.dma_val[si]
            self.dma_val[si] += 16
            tok = (("d", si), self.dma_val[si])
        elif signal:
            self.cnt[eng] += 1
            tok = (("c", eng), self.cnt[eng])
        else:
            tok = (("c", eng), self.cnt[eng] + 1)
        self.last_sig[eng] = signal or dma
        waits = []
        wd = self.waited[eng]
        for k, v in deps.items():
            if wd.get(k, 0) >= v:
                continue
            wd[k] = v
            waits.append((k, v))
        k, v = tok
        for r in reads:
            if r.r.get(k, 0) < v:
                r.r[k] = v
        for w in writes:
            w.w = tok
            w.r = {}
        self.q[eng].append((waits, fn, tok, signal))
        if is_output:
            self.out_tokens.append(tok)
        return tok

    def emit(self, nc, st):
        csem = {e: st.enter_context(nc.semaphore("c_" + e)) for e in ENGS}
        dsem = [st.enter_context(nc.semaphore("d_%d" % i)) for i in range(N_DMA_SEMS)]
        block = st.enter_context(nc.Block())

        def sem_of(k):
            return csem[k[1]] if k[0] == "c" else dsem[k[1]]

        def run(engname, engine):
            assert self.last_sig[engname], engname
            for waits, fn, tok, signal in self.q[engname]:
                for k, v in waits:
                    engine.wait_ge(sem_of(k), v)
                ins = fn(engine)
                k, v = tok
                if k[0] == "c":
                    if signal:
                        ins.then_inc(csem[k[1]], 1)
                else:
                    ins.then_inc(dsem[k[1]], 16)
            if engname == "sync":
                final = {}
                for k, v in self.out_tokens:
                    if final.get(k, 0) < v:
                        final[k] = v
                for k, v in final.items():
                    engine.wait_ge(sem_of(k), v)

        @block.sync
        def _(e):
            run("sync", e)

        @block.scalar
        def _(e):
            run("scalar", e)

        @block.gpsimd
        def _(e):
            run("gpsimd", e)

        @block.vector
        def _(e):
            run("vector", e)

        @block.tensor
        def _(e):
            run("tensor", e)


def ap_bc_mid(ap, n):
    a = [list(x) for x in ap.ap]
    return AP(ap.tensor, ap.offset, [a[0], [0, n]] + a[1:])


def ap_bc_last(ap, n):
    a = [list(x) for x in ap.ap]
    return AP(ap.tensor, ap.offset, a + [[0, n]])


def lambda_init(layer):
    return 0.8 - 0.6 * math.exp(-0.3 * layer)


C_LRUX, C_LRUG, C_GQ, C_GK, C_GV, C_GG, C_DQ, C_DK, C_DV, C_DG = (
    0, 256, 512, 768, 896, 1024, 1280, 1792, 2304, 2816)


def build():
    nc = bass.Bass("TRN2", target_bir_lowering=False)
    P = Prog()
    st = ExitStack()

    def din(name, shape):
        return nc.dram_tensor(name, list(shape), F32, kind="ExternalInput").ap()

    def dout(name, shape):
        return nc.dram_tensor(name, list(shape), F32, kind="ExternalOutput").ap()

    xin = [din("xp", [1024, 1024]), din("xs", [1024, 1024])]
    cgk = din("cgk", [2, 512, 128]); cgv = din("cgv", [2, 512, 128])
    cdk = din("cdk", [2, 512, 512]); cdv = din("cdv", [2, 512, 512])

    w_mod = din("w_mod", [2, 1024, 3072])
    w_in = din("w_in", [2, 1024, 3328]); w_out = din("w_out", [2, 1024, 1024])
    gq = din("gqa_gq", [2, 64]); gk = din("gqa_gk", [2, 64])
    dlam = din("diff_lam", [2, 4, 64])
    ident = din("ident", [128, 128]); ropec = din("ropec", [1024, 64]); ropes = din("ropes", [1024, 64])
    pvec = din("pvec", [128, NPV]); wbd_in = din("wbd", [128, 2 * 8 * 128])

    yout = [dout("yp", [1024, 1024]), dout("ys", [1024, 1024])]
    ngk = dout("ngk", [4, 2, 256, 128]); ngv = dout("ngv", [4, 2, 256, 128])
    ndk = dout("ndk", [4, 2, 256, 512]); ndv = dout("ndv", [4, 2, 256, 512])
    nlru = dout("nlru", [4, 2, 2, 256])

    def sb(name, shape, dt):
        return st.enter_context(nc.sbuf_tensor(name, list(shape), dt))

    X = sb("X", [128, 8, 1024], F32); RX = [Res() for _ in range(8)]
    WIN = sb("WIN", [128, 8, 3328], BF16)
    RW_LRU = Res(); RW_GQA = Res(); RW_DIFF = [Res() for _ in range(4)]
    WOUT = sb("WOUT", [128, 8, 1024], BF16); RW_OUT = Res()
    HT = sb("HT", [128, 8, 1024], BF16); RHT = [Res() for _ in range(8)]
    MIXT = sb("MIXT", [128, 8, 1024], BF16); RMIX = [[Res() for _ in range(8)] for _ in range(8)]

    IDF = sb("IDF", [128, 128], F32); IDB = sb("IDB", [128, 128], BF16); R_ID = Res()
    ROPEC = sb("ROPEC", [128, 8, 64], F32); ROPES = sb("ROPES", [128, 8, 64], F32); R_ROPE = Res()
    PVEC = sb("PVEC", [128, NPV], F32)
    _o = [0]

    def pv(n, pattern=None, **kw):
        v = PVEC[:, _o[0]:_o[0] + n]
        _o[0] += n
        return v.rearrange(pattern, **kw) if pattern else v
    CONDT = pv(16, "p (k c) -> p k c", c=2)
    BMT = pv(48, "p (l k) -> p l k", l=2)
    GPRE = pv(16, "p (l k) -> p l k", l=2)
    GPOST = pv(16, "p (l k) -> p l k", l=2)
    CW = pv(16, "p (l c j) -> p l c j", l=2, c=2)
    CB = pv(4, "p (l c) -> p l c", l=2)
    BA = pv(8, "p (l d c) -> p l d c", l=2, d=2)
    BXX = pv(8, "p (l d c) -> p l d c", l=2, d=2)
    LAMT = pv(8, "p (l d c) -> p l d c", l=2, d=2)
    H0 = pv(8, "p (l d c) -> p l d c", l=2, d=2)
    GSUBS = pv(2)
    assert _o[0] == NPV
    SC = sb("SC", [128, 8, 2], BF16); R_SC = Res()
    MODT = sb("MODT", [128, 2, 24, 2], F32); GS = sb("GS", [128, 2, 8, 2], F32); GPT = sb("GPT", [128, 2, 8, 2], F32)
    R_MOD = [Res(), Res()]
    CEXP = sb("CEXP", [128, 2, 2, 2], F32)
    WBD = sb("WBD", [128, 2, 8, 128], BF16)
    G6 = sb("G6", [128, 2, 6, 64], F32)
    DLS = sb("DLS", [128, 2, 4], F32)
    NLAM = sb("NLAM", [128, 2], F32)
    PAR = []

    def pres():
        r = Res()
        PAR.append(r)
        return r
    STP = sb("STP", [128, 32, 24], F32); RST = [Res() for _ in range(32)]
    st_i = [0]

    def new_stat():
        i = st_i[0] % 32
        st_i[0] += 1
        return STP[:, i, :], RST[i]

    GR = 64
    A_COLS = 15040
    ARENA = sb("ARENA", [128, A_COLS], F32)
    RGR = [Res() for _ in range(A_COLS // GR)]
    bump = [0]

    class Tile:
        def __init__(self, off, ncols):
            assert off % GR == 0
            self.off = off
            self.ncols = ncols
            self.res = RGR[off // GR:(off + ncols + GR - 1) // GR]

        def f32(self):
            return ARENA[:, self.off:self.off + self.ncols]

        def bf(self):
            return ARENA[:, self.off:self.off + self.ncols].bitcast(BF16)

    def alloc(ncols):
        off = bump[0]
        n = ((ncols + GR - 1) // GR) * GR
        bump[0] += n
        assert bump[0] <= A_COLS, bump[0]
        return Tile(off, ncols)

    XPAD = alloc(1088); U = alloc(1024); RR = alloc(1024); II = alloc(1024); TT = alloc(1024)
    Y0 = alloc(1024); Y1 = alloc(1024); UBF = alloc(512)
    XN = U; JUNK = Tile(RR.off, 512)
    GPB = II; TMP = TT; JUNK2 = Tile(Y0.off, 512); BCT = [Tile(Y1.off, 128), Tile(Y1.off + 128, 128)]
    QT = alloc(1024)
    KT = alloc(768)
    VA = alloc(1024)
    GATE = alloc(1024)
    assert KT.off == QT.off + 1024 and VA.off == QT.off + 1792 and GATE.off == QT.off + 2816
    PTB = [alloc(256) for _ in range(3)]
    QK = alloc(384); T1 = alloc(384)
    QKB = [alloc(192), alloc(192)]
    T2 = alloc(384)
    CKB = alloc(256)
    assert CKB.off == T2.off + 384
    OST = [Tile(T2.off, 256), Tile(T2.off + 256, 256)]
    O0T = alloc(256); DOT = alloc(256)
    DOBT = alloc(128)
    OBT = alloc(256)
    assert T1.off == QK.off + 384 and CKB.off == T2.off + 384
    assert DOT.off == O0T.off + 256 and DOBT.off == O0T.off + 512
    GQT = [(QK, T1, T2), (Tile(GATE.off, 384), Tile(GATE.off + 384, 384), Tile(O0T.off, 384))]
    T1S = [Tile(QK.off, 256), Tile(QK.off + 256, 256)]
    T2S = [Tile(T2.off, 256), Tile(T2.off + 256, 256)]
    WM = [Tile(QT.off, 2048), Tile(QT.off + 2048, 2048), Tile(QT.off + 4096, 2048)]
    WBDF_T = Tile(U.off, 2048); DLB_T = Tile(II.off, 512); DLP_T = Tile(II.off + 512, 256)
    GQB_T = Tile(II.off + 768, 128); GKB_T = Tile(II.off + 896, 128)
    WBDF = WBDF_T.f32().rearrange("p (l i c) -> p l i c", l=2, i=8)
    DLB = DLB_T.f32().rearrange("p (l r d) -> p l r d", l=2, r=4)
    DLP = DLP_T.f32().rearrange("p (l r d) -> p l r d", l=2, r=2)
    GQB = GQB_T.f32().rearrange("p (l d) -> p l d", l=2)
    GKB = GKB_T.f32().rearrange("p (l d) -> p l d", l=2)

    PS = st.enter_context(nc.psum_tensor("PS", [128, 4096], F32))
    RPB = [Res(x=True) for _ in range(8)]

    def bank(b):
        return PS[:, b * 512:(b + 1) * 512]

    def bankbf(b):
        return PS[:, b * 512:(b + 1) * 512].bitcast(BF16)

    def dbank(d):
        return PS[:, d * 1024:(d + 1) * 1024]

    def rdb(d):
        return [RPB[2 * d], RPB[2 * d + 1]]

    def V(fn, r=(), w=()):
        return P.op("vector", fn, r, w)

    def A(fn, r=(), w=()):
        return P.op("scalar", fn, r, w)

    def G(fn, r=(), w=()):
        return P.op("gpsimd", fn, r, w)

    def T(fn, r=(), w=(), sig=True):
        return P.op("tensor", fn, r, w, signal=sig)

    def D(fn, r=(), w=(), out=False):
        return P.op("sync", fn, r, w, dma=True, is_output=out)

    def DG(fn, r=(), w=()):
        return P.op("gpsimd", fn, r, w, dma=True)

    def load_win_lru(l):
        src = w_in[l].rearrange("(kc p) c -> p kc c", p=128)
        DG(lambda e: e.dma_start(out=WIN[:, :, 0:512], in_=src[:, :, 0:512]), w=[RW_LRU])

    def load_win_gqa(l):
        src = w_in[l].rearrange("(kc p) c -> p kc c", p=128)
        DG(lambda e: e.dma_start(out=WIN[:, :, 512:1280], in_=src[:, :, 512:1280]), w=[RW_GQA])

    def load_win_diff(l, h):
        src = w_in[l].rearrange("(kc p) c -> p kc c", p=128)
        for seg in range(4):
            c0 = C_DQ + seg * 512 + h * 128
            DG(lambda e, c0=c0: e.dma_start(out=WIN[:, :, c0:c0 + 128], in_=src[:, :, c0:c0 + 128]), w=[RW_DIFF[h]])

    def load_wout(l):
        src = w_out[l].rearrange("(kc p) c -> p kc c", p=128)
        DG(lambda e: e.dma_start(out=WOUT[:], in_=src), w=[RW_OUT])

    r_pv = pres()
    D(lambda e: e.dma_start(out=PVEC[:], in_=pvec), w=[r_pv])
    D(lambda e: e.dma_start(out=IDF[:], in_=ident), w=[R_ID])
    DG(lambda e: e.dma_start(out=IDB[:], in_=ident), w=[R_ID])
    A(lambda e: e.activation(SC[:], CONDT, AF.Silu), r=[r_pv], w=[R_SC])
    for tt in range(8):
        D(lambda e, tt=tt: e.dma_start(out=X[:, tt, :], in_=xin[0][tt * 128:(tt + 1) * 128, :]), w=[RX[tt]])
    D(lambda e: e.dma_start(out=ROPEC[:], in_=ropec.rearrange("(t p) d -> p t d", p=128)), w=[R_ROPE])
    D(lambda e: e.dma_start(out=ROPES[:], in_=ropes.rearrange("(t p) d -> p t d", p=128)), w=[R_ROPE])

    def small_load(out_ap, in_ap, extra=()):
        r = pres()
        D(lambda e: e.dma_start(out=out_ap, in_=in_ap, allow_slow_non_contiguous=True), w=[r] + list(extra))
        return r

    r_gqb = []; r_gkb = []; r_dlb = []
    for l in range(2):
        r_gqb.append(small_load(GQB[:, l, :], gq[l].partition_broadcast(128), GQB_T.res))
        r_gkb.append(small_load(GKB[:, l, :], gk[l].partition_broadcast(128), GKB_T.res))
        r_dlb.append(small_load(DLB[:, l, :, :], dlam[l].partition_broadcast(128), DLB_T.res))
    r_wbd = pres()
    DG(lambda e: e.dma_start(out=WBD[:].rearrange("p l i c -> p (l i c)"), in_=wbd_in), w=[r_wbd])
    r_cexp = pres()
    A(lambda e: e.activation(CEXP[:], LAMT, AF.Exp, scale=-1.0), r=[r_pv], w=[r_cexp])
    A(lambda e: e.activation(CEXP[:], CEXP[:], AF.Ln, scale=1.0, bias=1.0), r=[r_cexp], w=[r_cexp])
    V(lambda e: e.tensor_scalar(out=CEXP[:], in0=CEXP[:], scalar1=-8.0, scalar2=None, op0=ALU.mult), r=[r_cexp], w=[r_cexp])
    r_g6 = pres()
    for l in range(2):
        V(lambda e, l=l: e.tensor_copy(G6[:, l, 0:4, :], ap_bc_mid(GQB[:, l, :], 4)), r=r_gqb + GQB_T.res, w=[r_g6])
        V(lambda e, l=l: e.tensor_copy(G6[:, l, 4:6, :], ap_bc_mid(GKB[:, l, :], 2)), r=r_gkb + GKB_T.res, w=[r_g6])
    r_nl = pres()
    V(lambda e: e.tensor_tensor(DLP, DLB[:, :, 0::2, :], DLB[:, :, 1::2, :], ALU.mult), r=r_dlb + DLB_T.res, w=DLP_T.res)
    V(lambda e: e.tensor_reduce(out=DLS[:, :, 0:2], in_=DLP, axis=AX.X, op=ALU.add), r=DLP_T.res, w=[r_nl])
    A(lambda e: e.activation(DLS[:, :, 2:4], DLS[:, :, 0:2], AF.Exp), r=[r_nl], w=[r_nl])
    V(lambda e: e.tensor_tensor(NLAM[:], DLS[:, :, 3], DLS[:, :, 2], ALU.subtract), r=[r_nl], w=[r_nl])
    for l in range(2):
        V(lambda e, l=l: e.tensor_scalar(out=NLAM[:, l:l + 1], in0=NLAM[:, l:l + 1], scalar1=-lambda_init(l),
                                         scalar2=None, op0=ALU.add), r=[r_nl], w=[r_nl])
        V(lambda e, l=l: e.tensor_scalar(out=GSUBS[:, l:l + 1], in0=GSUBS[:, l:l + 1], scalar1=1.0 - lambda_init(l),
                                         scalar2=None, op0=ALU.mult), r=[r_pv], w=[r_pv])

    load_win_lru(0)
    wmi = [0]

    def mod(l, wms, psm_b):
        PSM = bank(psm_b)[:, 0:48].rearrange("p (f c) -> p f c", c=2)
        srcs = [w_mod[l].rearrange("(kc p) c -> p kc c", p=128)[:, :, blk * 512:(blk + 1) * 512] for blk in range(6)]

        def dma(blk):
            wm = wms[blk % len(wms)]
            DG(lambda e: e.dma_start(out=wm.bf().rearrange("p (k c) -> p k c", k=8), in_=srcs[blk]), w=wm.res)

        for blk in range(len(wms)):
            dma(blk)
        yield
        for blk in range(6):
            wm = wms[blk % len(wms)]
            wmv = wm.bf().rearrange("p (k c) -> p k c", k=8)
            for fcl in range(4):
                fc = blk * 4 + fcl
                for kc in range(8):
                    T(lambda e, fc=fc, kc=kc, fcl=fcl, wmv=wmv: e.matmul(
                        PSM[:, fc, :], wmv[:, kc, fcl * 128:(fcl + 1) * 128], SC[:, kc, :],
                        start=(kc == 0), stop=(kc == 7)), r=wm.res + [R_SC], w=[RPB[psm_b]], sig=(kc == 7))
            if blk + len(wms) < 6:
                dma(blk + len(wms))
            yield
        V(lambda e: e.tensor_tensor(MODT[:, l], PSM, ap_bc_last(BMT[:, l, :], 2), ALU.add),
          r=[RPB[psm_b]] + PAR, w=[R_MOD[l]])
        V(lambda e: e.scalar_tensor_tensor(out=GS[:, l], in0=MODT[:, l, 8:16, :], scalar=1.0,
                                           in1=ap_bc_last(GPRE[:, l, :], 2), op0=ALU.add, op1=ALU.mult),
          r=[R_MOD[l]] + PAR, w=[R_MOD[l]])
        V(lambda e: e.tensor_tensor(GPT[:, l], MODT[:, l, 16:24, :], ap_bc_last(GPOST[:, l, :], 2), ALU.mult),
          r=[R_MOD[l]] + PAR, w=[R_MOD[l]])
        yield

    for _ in mod(0, WM, 6):
        pass
    load_win_gqa(0)
    for h in range(4):
        load_win_diff(0, h)
    load_wout(0)

    steps = [(0, 0), (0, 1), (1, 0), (1, 1)]

    def rstd_from_sumsq(stat, rs, n_el, cols=1):
        A(lambda e: e.activation(stat[:, 8:8 + cols], stat[:, 0:cols], AF.Ln, scale=1.0 / n_el, bias=EPS), r=[rs], w=[rs])
        A(lambda e: e.activation(stat[:, 16:16 + cols], stat[:, 8:8 + cols], AF.Exp, scale=-0.5), r=[rs], w=[rs])

    def x_load(s, tt):
        DG(lambda e: e.dma_start(out=X[:, tt, :], in_=xin[s][tt * 128:(tt + 1) * 128, :]), w=[RX[tt]])

    XNS = [XN, Tile(XPAD.off, 1024)]
    JUNKS = [JUNK, Tile(RR.off + 512, 512)]

    def prenorm_tile(s, l, tt, XN=XN, JUNK=JUNK):
        c = s
        stat, rs = new_stat()
        A(lambda e: e.activation(JUNK.bf(), X[:, tt, :], AF.Square, accum_out=stat[:, 0:1]), r=[RX[tt]], w=JUNK.res + [rs])
        yield
        rstd_from_sumsq(stat, rs, 1024)
        yield
        V(lambda e: e.tensor_scalar(out=XN.f32(), in0=X[:, tt, :], scalar1=stat[:, 16:17], scalar2=None, op0=ALU.mult),
          r=[RX[tt], rs], w=XN.res)
        yield
        d = 2 + tt % 2
        pst = dbank(d)
        for kc in range(8):
            T(lambda e, kc=kc: e.transpose(pst[:, kc * 128:(kc + 1) * 128], XN.f32()[:, kc * 128:(kc + 1) * 128], IDF[:]),
              r=XN.res + [R_ID], w=[RPB[2 * d + kc // 4]], sig=(kc % 4 == 3))
        yield
        for kc in range(8):
            rr = [RPB[2 * d + kc // 4], R_MOD[l]]
            if kc < 4:
                A(lambda e, kc=kc: e.activation(HT[:, kc, tt * 128:(tt + 1) * 128], pst[:, kc * 128:(kc + 1) * 128], AF.Identity,
                                                scale=GS[:, l, kc, c:c + 1], bias=MODT[:, l, kc, c:c + 1]), r=rr, w=[RHT[tt]])
            else:
                V(lambda e, kc=kc: e.tensor_scalar(out=HT[:, kc, tt * 128:(tt + 1) * 128], in0=pst[:, kc * 128:(kc + 1) * 128],
                                                   scalar1=GS[:, l, kc, c:c + 1], scalar2=MODT[:, l, kc, c:c + 1],
                                                   op0=ALU.mult, op1=ALU.add), r=rr, w=[RHT[tt]])
            if kc % 2 == 1:
                yield

    def prenorm(s, l):
        for tt in range(8):
            for _ in prenorm_tile(s, l, tt):
                pass

    def proj_fm(col0, d, rw):
        ps = dbank(d)
        for half in range(2):
            for kc in range(8):
                T(lambda e, half=half, kc=kc: e.matmul(ps[:, half * 512:(half + 1) * 512], WIN[:, kc, col0:col0 + 128],
                                                       HT[:, kc, half * 512:(half + 1) * 512], start=(kc == 0), stop=(kc == 7)),
                  r=[rw] + RHT[half * 4:half * 4 + 4], w=[RPB[2 * d + half]], sig=(kc == 7))
        return ps

    def all_gates(s, l):
        cols = [(C_LRUG, RW_LRU), (C_LRUG + 128, RW_LRU), (C_GG, RW_GQA), (C_GG + 128, RW_GQA)] + \
               [(C_DG + h * 128, RW_DIFF[h]) for h in range(4)]
        for fc, (col0, rw) in enumerate(cols):
            d = fc % 2
            ps = proj_fm(col0, d, rw)
            yield
            A(lambda e, fc=fc, ps=ps: e.activation(MIXT[:, fc, :], ps, AF.Silu), r=rdb(d), w=RMIX[fc])
            yield

    def par_gen(*gens):
        alive = list(gens)
        while alive:
            for g in list(alive):
                try:
                    next(g)
                except StopIteration:
                    alive.remove(g)
            yield

    def lru(s, l):
        nseq, L = (4, 256) if s == 0 else (1, 1024)
        xpad = XPAD.f32()[:, 0:nseq * (L + 3)].rearrange("p (q t) -> p q t", q=nseq)
        XR = Tile(XPAD.off, 1024)
        bufs = [(RR, II, Y0, 1), (XR, TT, Y1, 0)]

        def v3(t):
            return t.f32().rearrange("p (q t) -> p q t", q=nseq)

        for ch in range(2):
            G(lambda e: e.memset(xpad[:, :, 0:2], 0.0), w=XPAD.res)
            G(lambda e: e.memset(xpad[:, :, L + 2:L + 3], 0.0), w=XPAD.res)
            yield
            ps = proj_fm(C_LRUX + ch * 128, 0, RW_LRU)
            yield
            A(lambda e, ps=ps: e.copy(xpad[:, :, 2:2 + L], ps.rearrange("p (q t) -> p q t", q=nseq)), r=rdb(0), w=XPAD.res)
            yield
            V(lambda e, ch=ch: e.tensor_scalar(out=v3(U), in0=xpad[:, :, 0:L], scalar1=CW[:, l, ch, 0:1],
                                               scalar2=CB[:, l, ch:ch + 1], op0=ALU.mult, op1=ALU.add),
              r=XPAD.res + PAR, w=U.res)
            yield
            for j in range(1, 4):
                V(lambda e, ch=ch, j=j: e.scalar_tensor_tensor(out=v3(U), in0=xpad[:, :, j:j + L], scalar=CW[:, l, ch, j:j + 1],
                                                               in1=v3(U), op0=ALU.mult, op1=ALU.add),
                  r=XPAD.res + U.res + PAR, w=U.res)
                yield
            A(lambda e: e.copy(UBF.bf(), U.f32()), r=U.res, w=UBF.res)
            yield

            def direction(d, ch=ch):
                R, I, Y, pd = bufs[d]
                for typ in range(2):
                    wi = d * 4 + typ * 2 + ch
                    for half in range(2):
                        T(lambda e, wi=wi, half=half: e.matmul(
                            dbank(pd)[:, half * 512:(half + 1) * 512], WBD[:, l, wi, :], UBF.bf()[:, half * 512:(half + 1) * 512],
                            start=True, stop=True), r=UBF.res + PAR, w=[RPB[2 * pd + half]])
                    yield
                    dst = R if typ == 0 else I
                    bias = (BA if typ == 0 else BXX)[:, l, d, ch:ch + 1]
                    A(lambda e, dst=dst, bias=bias: e.activation(dst.f32(), dbank(pd), AF.Sigmoid, bias=bias),
                      r=rdb(pd) + PAR, w=dst.res)
                    yield
                A(lambda e: e.activation(R.f32(), R.f32(), AF.Exp, scale=CEXP[:, l, d, ch:ch + 1]), r=R.res + PAR, w=R.res)
                yield
                V(lambda e: e.tensor_tensor(Y.f32(), R.f32(), R.f32(), ALU.mult), r=R.res, w=Y.res)
                yield
                A(lambda e: e.activation(Y.f32(), Y.f32(), AF.Sqrt, scale=-1.0, bias=1.0), r=Y.res, w=Y.res)
                yield
                if s == 0:
                    pos = 0 if d == 0 else L - 1
                    V(lambda e, pos=pos: e.memset(v3(R)[:, :, pos:pos + 1], 0.0), r=R.res, w=R.res)
                V(lambda e: e.tensor_tensor(I.f32(), I.f32(), U.f32(), ALU.mult), r=I.res + U.res, w=I.res)
                yield
                V(lambda e: e.tensor_tensor(I.f32(), I.f32(), Y.f32(), ALU.mult), r=I.res + Y.res, w=I.res)
                yield
                init = 0.0 if s == 0 else H0[:, l, d, ch:ch + 1]
                if d == 0:
                    V(lambda e: e.tensor_tensor_scan(Y.f32(), R.f32(), I.f32(), init, ALU.mult, ALU.add),
                      r=R.res + I.res + PAR, w=Y.res)
                else:
                    V(lambda e: e.tensor_tensor_scan(Y.f32()[:, ::-1], R.f32()[:, ::-1], I.f32()[:, ::-1],
                                                     init, ALU.mult, ALU.add), r=R.res + I.res + PAR, w=Y.res)
                yield
                if s == 0:
                    pos = L - 1 if d == 0 else 0
                    srcv = v3(Y)[:, :, pos]
                    dstv = nlru[:, l, d, ch * 128:(ch + 1) * 128].rearrange("q p -> p q")
                    D(lambda e: e.dma_start(out=dstv, in_=srcv, allow_slow_non_contiguous=True), r=Y.res, out=True)

            yield from par_gen(direction(0), direction(1))
            V(lambda e: e.tensor_tensor(Y0.f32(), Y0.f32(), Y1.f32(), ALU.add), r=Y0.res + Y1.res, w=Y0.res)
            yield
            V(lambda e, ch=ch: e.tensor_tensor(MIXT[:, ch, :], Y0.f32(), MIXT[:, ch, :], ALU.mult), r=Y0.res + RMIX[ch], w=RMIX[ch])
            yield

    pt_i = [0]

    def rope_views(src_ap, nh, tt):
        x5 = src_ap.rearrange("p (h b f d) -> p h b f d", h=nh, b=2, f=2)
        xs5 = x5[:, :, :, ::-1, :]
        c5 = ap_bc_mid(ROPEC[:, tt, :].rearrange("p (b f d) -> p b f d", b=2, f=2), nh)
        s5 = ap_bc_mid(ROPES[:, tt, :].rearrange("p (b f d) -> p b f d", b=2, f=2), nh)
        return x5, xs5, c5, s5

    PJ_B, TR_B = 6, 7

    def interleave(gens, weights=None):
        alive = [[g, (weights[i] if weights else 1)] for i, g in enumerate(gens) if g is not None]
        while alive:
            for item in list(alive):
                g, w = item
                for _ in range(w):
                    try:
                        next(g)
                    except StopIteration:
                        alive.remove(item)
                        break

    def attn_core(s, qb, qt_ap, kt_ap, va_of, W, rd_extra):
        chunks = [2 * qb, 2 * qb + 1] if s == 0 else list(range(12))
        n = len(chunks)
        W1 = W + 1
        pts = {}

        def scores(ci):
            kc = chunks[ci]
            b0 = 2 if s == 0 else 2 + 2 * (ci % 2)
            for m in range(2):
                T(lambda e, m=m, kc=kc, b0=b0: e.matmul(bank(b0 + m)[:, 0:256], kt_ap[m * 64:(m + 1) * 64, kc, :],
                                                        qt_ap[m * 64:(m + 1) * 64, qb * 256:(qb + 1) * 256], start=True, stop=True),
                  r=rd_extra, w=[RPB[b0 + m]], sig=(m == 1))
            pt = PTB[pt_i[0] % len(PTB)]; pt_i[0] += 1
            pts[ci] = pt
            sc2 = PS[:, b0 * 512:(b0 + 2) * 512].rearrange("p (m c) -> p m c", m=2)[:, :, 0:256]
            A(lambda e, sc2=sc2, pt=pt: e.activation(pt.bf().rearrange("p (m c) -> p m c", m=2), sc2, AF.Exp, scale=0.125),
              r=[RPB[b0], RPB[b0 + 1]], w=pt.res)

        def pv(ci):
            kc = chunks[ci]
            pt = pts[ci]
            for m in range(2):
                for qt in range(2):
                    last = (m == 1 and qt == 1)
                    T(lambda e, m=m, qt=qt, kc=kc, pt=pt, ci=ci: e.matmul(
                        bank(m)[:, qt * W1:(qt + 1) * W1], pt.bf()[:, m * 256 + qt * 128:m * 256 + (qt + 1) * 128],
                        va_of(m)[:, kc, 0:W1], start=(ci == 0 and qt == 0), stop=(ci == n - 1),
                        skip_group_check=True), r=pt.res + rd_extra, w=[RPB[m]], sig=last)

        scores(0)
        yield
        for ci in range(n):
            if ci + 1 < n:
                scores(ci + 1)
            pv(ci)
            yield

    class BufSet:
        def __init__(self, base):
            self.QK = Tile(base, 1792)
            self.VA = Tile(base + 1792, 1024)
            self.GATE = Tile(base + 2816, 1024)

    BS_A = BufSet(QT.off)
    BS_B = BufSet(II.off)
    _gb = II.off + 2816
    QKBD = QKB + [Tile(_gb, 192), Tile(_gb + 192, 192)]
    OSTD = OST + [Tile(_gb + 384, 256), Tile(_gb + 640, 256)]

    def proj_gate(col0, rw, dst_of_half, gres):
        for half in range(2):
            for kc in range(8):
                T(lambda e, half=half, kc=kc: e.matmul(bank(PJ_B), WIN[:, kc, col0:col0 + 128],
                                                       HT[:, kc, half * 512:(half + 1) * 512], start=(kc == 0), stop=(kc == 7)),
                  r=[rw] + RHT[half * 4:half * 4 + 4], w=[RPB[PJ_B]], sig=(kc == 7))
            yield
            A(lambda e, half=half: e.activation(dst_of_half(half), bank(PJ_B), AF.Silu), r=[RPB[PJ_B]], w=gres)
            yield

    def cache_k_load(src_ap):
        DG(lambda e: e.dma_start(out=CKB.bf().rearrange("p (c d) -> p c d", c=4), in_=src_ap), w=CKB.res)
        ptr = bankbf(PJ_B)
        for c4 in range(4):
            T(lambda e, c4=c4: e.transpose(ptr[:, c4 * 128:(c4 + 1) * 128], CKB.bf()[:, c4 * 128:(c4 + 1) * 128], IDB[:]),
              r=CKB.res + [R_ID], w=[RPB[PJ_B]], sig=(c4 == 3))
        return ptr

    def gqa_unit(s, l, bs):
        koff = 0 if s == 0 else 4
        qkall = bs.QK.bf()
        qt2 = qkall[:, 0:2048].rearrange("p (u t) -> p u t", u=2)
        kt = qkall[:, 2048:2048 + 1536].rearrange("p (k t) -> p k t", t=128)
        va = bs.VA.bf()[:, 0:2 * 12 * 66].rearrange("p (v k w) -> p v k w", v=2, k=12)
        allres = bs.QK.res + bs.VA.res

        def prep_head():
            G(lambda e: e.memset(va[:, :, :, 64:65], 1.0), w=bs.VA.res)
            yield
            if s == 1:
                for kv in range(2):
                    DG(lambda e, kv=kv: e.dma_start(out=va[:, kv, 0:4, 0:64],
                                                    in_=cgv[l].rearrange("(c p) (v d) -> p v c d", p=128, v=2)[:, kv]), w=bs.VA.res)
                ptr = cache_k_load(cgk[l].rearrange("(c p) d -> p c d", p=128))
                yield
                A(lambda e: e.copy(qkall[:, 2048:2048 + 512], ptr[:, 0:512]), r=[RPB[PJ_B]], w=bs.QK.res)
                yield

        def prep_tile(tt):
            par = tt % 2
            PJ = 6 if par == 0 else 4
            TRB = PJ + 1
            pj = bank(PJ)
            QK, T1, T2 = GQT[par]
            for kc in range(8):
                T(lambda e, kc=kc: e.matmul(pj[:, 0:512], HT[:, kc, tt * 128:(tt + 1) * 128], WIN[:, kc, 512:1024],
                                            start=(kc == 0), stop=(kc == 7)), r=[RHT[tt], RW_GQA], w=[RPB[PJ]], sig=(kc == 7))
            yield
            stat, rs = new_stat()
            A(lambda e: e.activation(T1.f32(), pj[:, 0:384], AF.Square), r=[RPB[PJ]], w=T1.res)
            yield
            V(lambda e: e.tensor_reduce(out=stat[:, 0:6], in_=T1.f32().rearrange("p (h d) -> p h d", h=6),
                                        axis=AX.X, op=ALU.add), r=T1.res, w=[rs])
            yield
            rstd_from_sumsq(stat, rs, 64, cols=6)
            yield
            qk3 = QK.f32().rearrange("p (h d) -> p h d", h=6)
            V(lambda e: e.tensor_tensor(qk3, pj[:, 0:384].rearrange("p (h d) -> p h d", h=6),
                                        ap_bc_last(stat[:, 16:22], 64), ALU.mult), r=[RPB[PJ], rs], w=QK.res)
            yield
            A(lambda e: e.copy(va[:, :, koff + tt, 0:64], pj[:, 384:512].rearrange("p (v d) -> p v d", v=2)),
              r=[RPB[PJ]], w=bs.VA.res)
            ost = OST[tt % 2]
            if s == 0:
                A(lambda e: e.copy(ost.f32()[:, 0:128], pj[:, 384:512]), r=[RPB[PJ]], w=ost.res)
            yield
            V(lambda e: e.tensor_tensor(qk3, qk3, G6[:, l], ALU.mult), r=QK.res + PAR, w=QK.res)
            yield
            qkb = QKB[par]
            qdst = qkb.bf()[:, 0:256].rearrange("p (b a d) -> p a b d", a=2, b=2)
            kdst = qkb.bf()[:, 256:384]
            if s == 0:
                seq, tq = tt // 2, tt % 2
                D(lambda e: e.dma_start(out=ngk[seq, l, tq * 128:(tq + 1) * 128, :], in_=QK.f32()[:, 256:384]),
                  r=QK.res, out=True)
                D(lambda e: e.dma_start(out=ngv[seq, l, tq * 128:(tq + 1) * 128, :], in_=ost.f32()[:, 0:128]),
                  r=ost.res, out=True)
                V(lambda e: e.tensor_copy(qdst, QK.f32()[:, 0:256].rearrange("p (a b d) -> p a b d", a=2, b=2)),
                  r=QK.res, w=qkb.res)
                V(lambda e: e.tensor_copy(kdst, QK.f32()[:, 256:384]), r=QK.res, w=qkb.res)
                yield
            else:
                x5, xs5, c5, s5 = rope_views(QK.f32(), 6, tt)
                t15 = T1.f32().rearrange("p (h b f d) -> p h b f d", h=6, b=2, f=2)
                t25 = T2.f32().rearrange("p (h b f d) -> p h b f d", h=6, b=2, f=2)
                V(lambda e: e.tensor_tensor(t15, x5, c5, ALU.mult), r=QK.res + [R_ROPE], w=T1.res)
                yield
                V(lambda e: e.tensor_tensor(t25, xs5, s5, ALU.mult), r=QK.res + [R_ROPE], w=T2.res)
                yield
                G(lambda e: e.tensor_tensor(qdst, T1.f32()[:, 0:256].rearrange("p (a b d) -> p a b d", a=2, b=2),
                                            T2.f32()[:, 0:256].rearrange("p (a b d) -> p a b d", a=2, b=2), ALU.add),
                  r=T1.res + T2.res, w=qkb.res)
                G(lambda e: e.tensor_tensor(kdst, T1.f32()[:, 256:384], T2.f32()[:, 256:384], ALU.add),
                  r=T1.res + T2.res, w=qkb.res)
                yield
            ptr = bankbf(TRB)
            for j in range(3):
                T(lambda e, j=j: e.transpose(ptr[:, j * 128:(j + 1) * 128], qkb.bf()[:, j * 128:(j + 1) * 128], IDB[:]),
                  r=qkb.res + [R_ID], w=[RPB[TRB]], sig=(j == 2))
            yield
            if s == 0:
                dst3 = AP(qkall.tensor, qkall[:, tt * 128:tt * 128 + 1].offset, [list(qkall.ap[0]), [1024, 3], [1, 128]])
                A(lambda e: e.copy(dst3, ptr[:, 0:384].rearrange("p (u t) -> p u t", u=3)), r=[RPB[TRB]], w=bs.QK.res)
            else:
                A(lambda e: e.copy(qt2[:, :, tt * 128:(tt + 1) * 128], ptr[:, 0:256].rearrange("p (u t) -> p u t", u=2)),
                  r=[RPB[TRB]], w=bs.QK.res)
                A(lambda e: e.copy(kt[:, koff + tt, :], ptr[:, 256:384]), r=[RPB[TRB]], w=bs.QK.res)
            yield

        def phases(qb):
            lst = []
            for u in range(2):
                def head(u=u):
                    stat, rs = new_stat()
                    for m in range(2):
                        accv = bank(m)[:, 0:130].rearrange("p (q w) -> p q w", q=2)
                        V(lambda e, m=m, accv=accv: e.reciprocal(stat[:, 2 * m:2 * m + 2], accv[:, :, 64]), r=[RPB[m]], w=[rs])
                    for m in range(2):
                        hd = u + 2 * m
                        accv = bank(m)[:, 0:130].rearrange("p (q w) -> p q w", q=2)
                        obv = OBT.bf().rearrange("p (q h d) -> p q h d", q=2, h=4)[:, :, hd, :]
                        V(lambda e, m=m, accv=accv, obv=obv: e.tensor_tensor(
                            obv, accv[:, :, 0:64], ap_bc_last(stat[:, 2 * m:2 * m + 2], 64), ALU.mult),
                          r=[RPB[m], rs], w=OBT.res)
                lst.append((attn_core(s, qb, qt2[:, u, :], kt, lambda m: va[:, m], 64, allres), head))
            return lst

        def tail(qb):
            pso = bankbf(2)[:, 512:1024]
            for qt in range(2):
                for pair in range(2):
                    T(lambda e, qt=qt, pair=pair: e.transpose(pso[:, (pair * 2 + qt) * 128:(pair * 2 + qt + 1) * 128],
                                                              OBT.bf()[:, qt * 256 + pair * 128:qt * 256 + (pair + 1) * 128], IDB[:]),
                      r=OBT.res + [R_ID], w=[RPB[2]], sig=(qt == 1 and pair == 1))
            yield
            for pair in range(2):
                V(lambda e, pair=pair: e.tensor_tensor(MIXT[:, 2 + pair, qb * 256:(qb + 1) * 256], pso[:, pair * 256:(pair + 1) * 256],
                                                       MIXT[:, 2 + pair, qb * 256:(qb + 1) * 256], ALU.mult),
                  r=[RPB[2], RMIX[2 + pair][2 * qb], RMIX[2 + pair][2 * qb + 1]],
                  w=[RMIX[2 + pair][2 * qb], RMIX[2 + pair][2 * qb + 1]])
            yield

        return prep_head, prep_tile, phases, tail

    def diff_unit(s, l, h, bs):
        NPAR = 4 if s == 0 else 2
        koff = 0 if s == 0 else 4
        qkall = bs.QK.bf()
        qt = qkall[:, 0:1024]
        kt = qkall[:, 1024:1024 + 1536].rearrange("p (k t) -> p k t", t=128)
        va = bs.VA.bf()[:, 0:12 * 130].rearrange("p (k w) -> p k w", k=12)
        allres = bs.QK.res + bs.VA.res
        rhs_all = WIN[:, :, C_DQ:C_DQ + 1536].rearrange("p k (s c) -> p k s c", s=3)[:, :, :, h * 128:(h + 1) * 128]

        def prep_head():
            G(lambda e: e.memset(va[:, :, 128:129], 1.0), w=bs.VA.res)
            yield
            if s == 1:
                DG(lambda e: e.dma_start(out=va[:, 0:4, 0:128],
                                         in_=cdv[l].rearrange("(c p) d -> p c d", p=128)[:, :, h * 128:(h + 1) * 128]), w=bs.VA.res)
                ptr = cache_k_load(cdk[l].rearrange("(c p) d -> p c d", p=128)[:, :, h * 128:(h + 1) * 128])
                yield
                A(lambda e: e.copy(qkall[:, 1024:1024 + 512], ptr[:, 0:512]), r=[RPB[PJ_B]], w=bs.QK.res)
                yield

        def prep_tile(tt):
            par = tt % NPAR
            PJ = (6, 7, 4, 5)[par]
            pj = bank(PJ)
            t1 = T1S[par % 2]; t2 = T2S[par % 2]
            for kc in range(8):
                T(lambda e, kc=kc: e.matmul(pj[:, 0:384], HT[:, kc, tt * 128:(tt + 1) * 128], rhs_all[:, kc],
                                            start=(kc == 0), stop=(kc == 7)), r=[RHT[tt], RW_DIFF[h]], w=[RPB[PJ]], sig=(kc == 7))
            yield
            qkb = QKBD[par]
            if s == 0:
                seq, tq = tt // 2, tt % 2
                ost = OSTD[par]
                A(lambda e: e.copy(ost.f32(), pj[:, 128:384]), r=[RPB[PJ]], w=ost.res)
                yield
                V(lambda e: e.tensor_copy(qkb.bf()[:, 0:256], pj[:, 0:256]), r=[RPB[PJ]], w=qkb.res)
                yield
                D(lambda e: e.dma_start(out=ndk[seq, l, tq * 128:(tq + 1) * 128, h * 128:(h + 1) * 128],
                                        in_=ost.f32()[:, 0:128]), r=ost.res, out=True)
                D(lambda e: e.dma_start(out=ndv[seq, l, tq * 128:(tq + 1) * 128, h * 128:(h + 1) * 128],
                                        in_=ost.f32()[:, 128:256]), r=ost.res, out=True)
            else:
                x5, xs5, c5, s5 = rope_views(pj[:, 0:256], 4, tt)
                t15 = t1.f32().rearrange("p (h b f d) -> p h b f d", h=4, b=2, f=2)
                t25 = t2.f32().rearrange("p (h b f d) -> p h b f d", h=4, b=2, f=2)
                V(lambda e: e.tensor_tensor(t15, x5, c5, ALU.mult), r=[RPB[PJ], R_ROPE], w=t1.res)
                yield
                V(lambda e: e.tensor_tensor(t25, xs5, s5, ALU.mult), r=[RPB[PJ], R_ROPE], w=t2.res)
                yield
                V(lambda e: e.tensor_tensor(qkb.bf()[:, 0:256], t1.f32(), t2.f32(), ALU.add), r=t1.res + t2.res, w=qkb.res)
                yield
            A(lambda e: e.copy(va[:, koff + tt, 0:128], pj[:, 256:384]), r=[RPB[PJ]], w=bs.VA.res)
            yield
            ptr = bankbf(PJ)[:, 768:1024]
            for j in range(2):
                T(lambda e, j=j: e.transpose(ptr[:, j * 128:(j + 1) * 128], qkb.bf()[:, j * 128:(j + 1) * 128], IDB[:]),
                  r=qkb.res + [R_ID], w=[RPB[PJ]], sig=(j == 1))
            yield
            dst2 = AP(qkall.tensor, qkall[:, tt * 128:tt * 128 + 1].offset, [list(qkall.ap[0]), [1024 + koff * 128, 2], [1, 128]])
            A(lambda e: e.copy(dst2, ptr[:, 0:256].rearrange("p (u t) -> p u t", u=2)), r=[RPB[PJ]], w=bs.QK.res)
            yield

        fin = {}

        def phases(qb):
            def head():
                acc0 = bank(0)[:, 0:258].rearrange("p (q w) -> p q w", q=2)
                acc1 = bank(1)[:, 0:258].rearrange("p (q w) -> p q w", q=2)
                stat, rs = new_stat()
                V(lambda e: e.reciprocal(stat[:, 0:2], acc0[:, :, 128]), r=[RPB[0]], w=[rs])
                V(lambda e: e.reciprocal(stat[:, 2:4], acc1[:, :, 128]), r=[RPB[1]], w=[rs])
                V(lambda e: e.tensor_scalar(out=stat[:, 4:6], in0=stat[:, 2:4], scalar1=NLAM[:, l:l + 1], scalar2=None, op0=ALU.mult),
                  r=[rs] + PAR, w=[rs])
                o0 = O0T.f32().rearrange("p (q d) -> p q d", q=2); do = DOT.f32().rearrange("p (q d) -> p q d", q=2)
                V(lambda e: e.tensor_tensor(o0, acc0[:, :, 0:128], ap_bc_last(stat[:, 0:2], 128), ALU.mult), r=[RPB[0], rs], w=O0T.res)
                V(lambda e: e.tensor_tensor(do, acc1[:, :, 0:128], ap_bc_last(stat[:, 4:6], 128), ALU.mult), r=[RPB[1], rs], w=DOT.res)
            return [(attn_core(s, qb, qt, kt, lambda m: va, 128, allres), head)]

        def tail(qb):
            o0 = O0T.f32().rearrange("p (q d) -> p q d", q=2); do = DOT.f32().rearrange("p (q d) -> p q d", q=2)
            V(lambda e: e.tensor_tensor(do, do, o0, ALU.add), r=DOT.res + O0T.res, w=DOT.res)
            yield
            stat2, rs2 = new_stat()
            for qi in range(2):
                A(lambda e, qi=qi: e.activation(O0T.f32()[:, qi * 128:(qi + 1) * 128], DOT.f32()[:, qi * 128:(qi + 1) * 128],
                                                AF.Square, accum_out=stat2[:, qi:qi + 1]), r=DOT.res, w=O0T.res + [rs2])
            yield
            rstd_from_sumsq(stat2, rs2, 128, cols=2)
            yield
            dob = DOBT.bf().rearrange("p (q d) -> p q d", q=2)
            V(lambda e: e.tensor_tensor(dob, do, ap_bc_last(stat2[:, 16:18], 128), ALU.mult), r=DOT.res + [rs2], w=DOBT.res)
            yield
            pso = bankbf(2)[:, 512:768]
            for qi in range(2):
                T(lambda e, qi=qi: e.transpose(pso[:, qi * 128:(qi + 1) * 128], DOBT.bf()[:, qi * 128:(qi + 1) * 128], IDB[:]),
                  r=DOBT.res + [R_ID], w=[RPB[2]], sig=(qi == 1))
            yield
            V(lambda e: e.scalar_tensor_tensor(out=MIXT[:, 4 + h, qb * 256:(qb + 1) * 256], in0=pso[:, 0:256], scalar=GSUBS[:, l:l + 1],
                                               in1=MIXT[:, 4 + h, qb * 256:(qb + 1) * 256], op0=ALU.mult, op1=ALU.mult),
              r=[RPB[2], RMIX[4 + h][2 * qb], RMIX[4 + h][2 * qb + 1]] + PAR, w=[RMIX[4 + h][2 * qb], RMIX[4 + h][2 * qb + 1]])
            yield

        return prep_head, prep_tile, phases, tail

    def attention(s, l, nxt, lru_gen):
        sets = [BS_A, BS_B]
        units = [gqa_unit(s, l, sets[0])] + [diff_unit(s, l, h, sets[(h + 1) % 2]) for h in range(4)]
        loads = [lambda: load_win_gqa(nxt)] + [lambda h=h: load_win_diff(nxt, h) for h in range(4)]

        def prep_gen(u, tiles=range(8), head=True):
            if head:
                yield from u[0]()
            for tt in tiles:
                yield from u[1](tt)

        def core_gen(u):
            pend = [None]
            kstep = 3 if s == 0 else 1

            def step_pending(n):
                for _ in range(n):
                    if pend[0] is None:
                        return
                    try:
                        next(pend[0])
                    except StopIteration:
                        pend[0] = None

            for qb in range(4):
                for attn_gen, head in u[2](qb):
                    for _ in attn_gen:
                        step_pending(kstep)
                        yield
                    step_pending(1000)
                    head()
                    yield
                pend[0] = u[3](qb)
            while pend[0] is not None:
                step_pending(1)
                yield

        for _ in units[0][0]():
            pass
        interleave([prep_gen(units[0], range(0, 8, 2), False), prep_gen(units[0], range(1, 8, 2), False), lru_gen])
        if nxt is not None:
            load_win_lru(nxt)
            loads[0]()
        for i, u in enumerate(units):
            nx = units[i + 1] if i + 1 < len(units) else None
            if nx is None:
                interleave([core_gen(u)])
            else:
                npar = 4 if s == 0 else 2
                interleave([core_gen(u)] + [prep_gen(nx, range(p_, 8, npar), p_ == 0) for p_ in range(npar)])
            if nx is not None and nxt is not None:
                loads[i + 1]()

    def out_proj(s, l, nstep):
        c = s
        psg = dbank(2)
        for kc in range(8):
            bc = BCT[kc % 2]
            V(lambda e, kc=kc, bc=bc: e.tensor_copy(bc.f32(), ap_bc_last(GPT[:, l, kc, c], 128)), r=[R_MOD[l]], w=bc.res)
            T(lambda e, kc=kc, bc=bc: e.matmul(psg[:, kc * 128:(kc + 1) * 128], bc.f32(), IDF[:], start=True, stop=True),
              r=bc.res + [R_ID], w=[RPB[4 + kc // 4]])
        A(lambda e: e.copy(GPB.f32(), psg), r=rdb(2), w=GPB.res)

        def mm(tt):
            d = tt % 2
            psy = dbank(d)
            for half in range(2):
                for fc in range(8):
                    T(lambda e, half=half, fc=fc: e.matmul(
                        psy[:, half * 512:(half + 1) * 512], MIXT[:, fc, tt * 128:(tt + 1) * 128], WOUT[:, fc, half * 512:(half + 1) * 512],
                        start=(fc == 0), stop=(fc == 7)), r=[RMIX[fc][tt], RW_OUT], w=[RPB[2 * d + half]], sig=(fc == 7))

        def post(tt):
            d = tt % 2
            psy = dbank(d)
            stat, rs = new_stat()
            A(lambda e: e.activation(JUNK2.bf(), psy, AF.Square, accum_out=stat[:, 0:1]), r=rdb(d), w=JUNK2.res + [rs])
            yield
            V(lambda e: e.tensor_tensor(TMP.f32(), psy, GPB.f32(), ALU.mult), r=rdb(d) + GPB.res, w=TMP.res)
            yield
            if tt + 2 < 8:
                mm(tt + 2)
            rstd_from_sumsq(stat, rs, 1024)
            yield
            V(lambda e: e.scalar_tensor_tensor(out=X[:, tt, :], in0=TMP.f32(), scalar=stat[:, 16:17], in1=X[:, tt, :],
                                               op0=ALU.mult, op1=ALU.add), r=TMP.res + [rs, RX[tt]], w=[RX[tt]])
            yield
            if l == 1:
                D(lambda e: e.dma_start(out=yout[s][tt * 128:(tt + 1) * 128, :], in_=X[:, tt, :]), r=[RX[tt]], out=True)
            yield

        posted = [-1]

        new_group = nstep is not None and nstep[1] == 0

        def post_chain():
            for tt in range(8):
                yield from post(tt)
                posted[0] = tt
                if new_group:
                    x_load(nstep[0], tt)

        def pn_chain(par):
            if nstep is None:
                return
            for tt in range(par, 8, 2):
                while posted[0] < tt:
                    yield
                yield from prenorm_tile(nstep[0], nstep[1], tt, XNS[par], JUNKS[par])

        mm(0)
        mm(1)
        for _ in post_chain():
            pass
        interleave([pn_chain(0), pn_chain(1)])

    prenorm(*steps[0])
    for si, (s, l) in enumerate(steps):
        nstep = steps[si + 1] if si + 1 < len(steps) else None
        nxt = nstep[1] if nstep is not None else None
        def gates_then_lru(s=s, l=l, si=si):
            g = mod(1, [Tile(U.off, 2048), Tile(II.off, 2048)], 0) if si == 0 else None
            if g is not None:
                next(g)
            yield from all_gates(s, l)
            if g is not None:
                yield from g
            yield from lru(s, l)

        attention(s, l, nxt, gates_then_lru())
        out_proj(s, l, nstep)
        if nxt is not None:
            load_wout(nxt)

    P.emit(nc, st)
    st.close()
    return nc


def _rope_tables():
    inv = np.power(np.float32(10000.0), -np.arange(16, dtype=np.float32) / np.float32(16)).astype(np.float32)
    t = np.arange(1024)
    row = (t // 64).astype(np.float32)
    col = (t % 64).astype(np.float32)
    ar = (row[:, None] * inv[None]).astype(np.float32)
    ac = (col[:, None] * inv[None]).astype(np.float32)
    C = np.concatenate([np.cos(ar), np.cos(ar), np.cos(ac), np.cos(ac)], axis=1).astype(np.float32)
    S = np.concatenate([-np.sin(ar), np.sin(ar), -np.sin(ac), np.sin(ac)], axis=1).astype(np.float32)
    return np.ascontiguousarray(C), np.ascontiguousarray(S)


_NC_CACHE = {}


def kernel(x_prompt, x_sample, cache_gqa_k, cache_gqa_v, cache_diff_k, cache_diff_v, state_lru,
           c, c_ctx, w_mod, b_mod, g_pre, g_post, w_in, w_out, lru_conv_w, lru_conv_b,
           lru_wa, lru_ba, lru_wx, lru_bx, lru_lambda, gqa_gq, gqa_gk, diff_lam, diff_gsub):
    f = lambda a: np.ascontiguousarray(np.asarray(a, dtype=np.float32))
    x_prompt, x_sample = f(x_prompt), f(x_sample)
    ropeC, ropeS = _rope_tables()
    shared = {
        "w_mod": f(w_mod), "b_mod": f(b_mod), "g_pre": f(g_pre), "g_post": f(g_post),
        "w_in": f(w_in), "w_out": f(w_out), "conv_w": f(lru_conv_w), "conv_b": f(lru_conv_b),
        "lru_wa": f(lru_wa), "lru_ba": f(lru_ba), "lru_wx": f(lru_wx), "lru_bx": f(lru_bx),
        "lru_lambda": f(lru_lambda), "gqa_gq": f(gqa_gq), "gqa_gk": f(gqa_gk),
        "diff_lam": f(diff_lam), "diff_gsub": f(diff_gsub),
        "ident": np.eye(128, dtype=np.float32), "ropec": ropeC, "ropes": ropeS,
    }
    c = f(c); c_ctx = f(c_ctx)
    fm = lambda a: np.moveaxis(a.reshape(a.shape[:-1] + (a.shape[-1] // 128, 128)), -1, 0)
    b_mod_, g_pre_, g_post_ = shared["b_mod"], shared["g_pre"], shared["g_post"]
    cw_ = fm(shared["conv_w"])
    common = [fm(b_mod_).reshape(128, -1), fm(g_pre_).reshape(128, -1), fm(g_post_).reshape(128, -1),
              np.transpose(cw_, (0, 1, 3, 2)).reshape(128, -1), fm(shared["conv_b"]).reshape(128, -1),
              fm(shared["lru_ba"]).reshape(128, -1), fm(shared["lru_bx"]).reshape(128, -1),
              fm(shared["lru_lambda"]).reshape(128, -1)]
    gsub_ = np.ascontiguousarray(shared["diff_gsub"].T)
    wbd = np.zeros((128, 2, 8, 128), np.float32)
    for typ, wsrc in enumerate((shared["lru_wa"], shared["lru_wx"])):
        for l_ in range(2):
            for d_ in range(2):
                for n_ in range(4):
                    ch_, b_ = n_ // 2, n_ % 2
                    wbd[b_ * 64:(b_ + 1) * 64, l_, d_ * 4 + typ * 2 + ch_, b_ * 64:(b_ + 1) * 64] = wsrc[l_, d_, n_]
    shared["wbd"] = wbd.reshape(128, -1)
    for k_ in ("b_mod", "g_pre", "g_post", "conv_w", "conv_b", "lru_wa", "lru_ba", "lru_wx", "lru_bx", "lru_lambda", "diff_gsub"):
        del shared[k_]
    in_maps = []
    for i in range(N_CORES):
        m = dict(shared)
        cond_i = np.stack([c_ctx, c[i]], axis=0)
        condT = np.transpose(fm(cond_i), (0, 2, 1)).reshape(128, -1)
        h0_ = fm(f(state_lru[i])).reshape(128, -1)
        m["pvec"] = np.ascontiguousarray(np.concatenate([condT] + common + [h0_, gsub_], axis=1), dtype=np.float32)
        assert m["pvec"].shape == (128, NPV)
        m["xp"] = x_prompt[4 * i:4 * i + 4].reshape(1024, 1024)
        m["xs"] = x_sample[i]
        m["cgk"] = f(cache_gqa_k[i]).reshape(2, 512, 128)
        m["cgv"] = f(cache_gqa_v[i]).reshape(2, 512, 128)
        m["cdk"] = f(cache_diff_k[i]).reshape(2, 512, 512)
        m["cdv"] = f(cache_diff_v[i]).reshape(2, 512, 512)
        in_maps.append(m)
    if "nc" not in _NC_CACHE:
        _NC_CACHE["nc"] = build()
    nc = _NC_CACHE["nc"]
    ncore = int(os.environ.get("KCORES", str(N_CORES)))
    res = run_bass_kernel_spmd(nc, in_maps[:ncore], core_ids=list(range(ncore)))
    R = list(res.results) + [res.results[0]] * (N_CORES - ncore)
    y_prompt = np.concatenate([R[i]["yp"].reshape(4, 256, 1024) for i in range(N_CORES)], axis=0)
    y_sample = np.stack([R[i]["ys"] for i in range(N_CORES)], axis=0)
    new_gqa_k = np.concatenate([R[i]["ngk"].reshape(4, 2, 256, 2, 64) for i in range(N_CORES)], axis=0)
    new_gqa_v = np.concatenate([R[i]["ngv"].reshape(4, 2, 256, 2, 64) for i in range(N_CORES)], axis=0)
    new_diff_k = np.concatenate([R[i]["ndk"].reshape(4, 2, 256, 4, 2, 64) for i in range(N_CORES)], axis=0)
    new_diff_v = np.concatenate([R[i]["ndv"].reshape(4, 2, 256, 4, 128) for i in range(N_CORES)], axis=0)
    new_lru = np.concatenate([R[i]["nlru"] for i in range(N_CORES)], axis=0)
    return (y_prompt.astype(np.float32), y_sample.astype(np.float32), new_gqa_k.astype(np.float32),
            new_gqa_v.astype(np.float32), new_diff_k.astype(np.float32), new_diff_v.astype(np.float32),
            new_lru.astype(np.float32))
```
